# Optimizing a Trainium2 kernel written in Bass

```python
import jax, jax.numpy as jnp
from jax import lax
import numpy as np

D_MODEL = 2048
BATCH = 8
SEQ = 2048
DEPTH = 2

N_MIXERS = 2
N_LAYERS_A = (DEPTH + 1) // 2
N_LAYERS_B = DEPTH // 2

CHUNK = 128
GM_WIDTH = D_MODEL
GM_GROUPS = 16
GM_GROUP_DIM = GM_WIDTH // GM_GROUPS

HEAD_DIM = 128
N_HEADS = D_MODEL // HEAD_DIM
N_KV_HEADS = 4
N_REP = N_HEADS // N_KV_HEADS
Q_BLOCK = 128
GRID_W = 64
AXIS_DIM = HEAD_DIM // 2
ROPE_THETA = 10000.0

D_FF = 4 * D_MODEL

NORM_EPS = 1e-6

kernel_name = "hybrid_gmlp_axial_gqa_encoder"


def rmsnorm(x, g):
    xf = x.astype(jnp.float32)
    xf = xf * lax.rsqrt(jnp.mean(xf * xf, axis=-1, keepdims=True) + NORM_EPS)
    return (xf * g.astype(jnp.float32)).astype(x.dtype)


def layernorm(x, g, b):
    xf = x.astype(jnp.float32)
    mu = jnp.mean(xf, axis=-1, keepdims=True)
    xc = xf - mu
    var = jnp.mean(xc * xc, axis=-1, keepdims=True)
    y = xc * lax.rsqrt(var + NORM_EPS) * g.astype(jnp.float32) + b.astype(jnp.float32)
    return y.astype(x.dtype)


def gmlp_mixer(h, w_in, ln_g, ln_b, w_s, b_s, w_out):
    B, S, _ = h.shape
    z = jax.nn.gelu(h @ w_in, approximate=False)
    u, v = z[..., :GM_WIDTH], z[..., GM_WIDTH:]
    v = layernorm(v, ln_g, ln_b)
    v = v.reshape(B, S // CHUNK, CHUNK, GM_GROUPS, GM_GROUP_DIM)
    s = jnp.einsum("gpq,bnqgc->bnpgc", w_s.astype(v.dtype), v)
    s = s + b_s.T.astype(v.dtype)[None, None, :, :, None]
    s = s.reshape(B, S, GM_WIDTH)
    return (u * s) @ w_out


def rope_axis(x, pos):
    half = AXIS_DIM // 2
    inv_freq = ROPE_THETA ** (-jnp.arange(0, AXIS_DIM, 2, dtype=jnp.float32) / AXIS_DIM)
    ang = pos[:, None] * inv_freq[None, :]
    cos, sin = jnp.cos(ang), jnp.sin(ang)
    xf = x.astype(jnp.float32)
    x1, x2 = xf[..., :half], xf[..., half:]
    out = jnp.concatenate([x1 * cos - x2 * sin, x2 * cos + x1 * sin], axis=-1)
    return out.astype(x.dtype)


def axial_rope(x, row, col):
    xt = jnp.swapaxes(x, 1, 2)
    xr = rope_axis(xt[..., :AXIS_DIM], row)
    xc = rope_axis(xt[..., AXIS_DIM:], col)
    return jnp.swapaxes(jnp.concatenate([xr, xc], axis=-1), 1, 2)


def attention_mixer(h, w_qkv, q_norm, k_norm, w_o):
    B, S, _ = h.shape
    ROWS = S // GRID_W
    row = jnp.repeat(jnp.arange(ROWS, dtype=jnp.float32), GRID_W)
    col = jnp.tile(jnp.arange(GRID_W, dtype=jnp.float32), ROWS)

    qkv = h @ w_qkv
    q_w = N_HEADS * HEAD_DIM
    kv_w = N_KV_HEADS * HEAD_DIM
    q = qkv[..., :q_w].reshape(B, S, N_HEADS, HEAD_DIM)
    k = qkv[..., q_w:q_w + kv_w].reshape(B, S, N_KV_HEADS, HEAD_DIM)
    v = qkv[..., q_w + kv_w:].reshape(B, S, N_KV_HEADS, HEAD_DIM)

    q = axial_rope(rmsnorm(q, q_norm), row, col)
    k = axial_rope(rmsnorm(k, k_norm), row, col)

    n_blocks = S // Q_BLOCK
    q = q.reshape(B, n_blocks, Q_BLOCK, N_KV_HEADS, N_REP, HEAD_DIM)
    q_blocks = jnp.transpose(q, (1, 0, 3, 4, 2, 5))
    k = jnp.transpose(k, (0, 2, 1, 3))
    v = jnp.transpose(v, (0, 2, 1, 3))
    scale = HEAD_DIM ** -0.5

    def one_block(qb):
        s = jnp.einsum("bgrqd,bgkd->bgrqk", qb, k).astype(jnp.float32) * scale
        p = jax.nn.softmax(s, axis=-1)
        return jnp.einsum("bgrqk,bgkd->bgrqd", p.astype(v.dtype), v)

    o = lax.map(one_block, q_blocks)
    o = jnp.transpose(o, (1, 0, 4, 2, 3, 5)).reshape(B, S, N_HEADS * HEAD_DIM)
    return o @ w_o


def sqrelu_mlp(h, w1, w2):
    a = jax.nn.relu(h @ w1)
    return (a * a) @ w2


def setup_inputs(seed: int = 0) -> dict:
    key = jax.random.key(seed)
    ks = jax.random.split(key, 20)
    f32 = jnp.float32

    def nrm(k, shape, scale):
        return jax.random.normal(k, shape, f32) * scale

    x = jax.random.normal(ks[0], (BATCH, SEQ, D_MODEL), f32)

    gm_w_in = nrm(ks[1], (N_LAYERS_A, D_MODEL, 2 * GM_WIDTH), D_MODEL ** -0.5)
    gm_ln_g = 1.0 + nrm(ks[2], (N_LAYERS_A, GM_WIDTH), 0.02)
    gm_ln_b = nrm(ks[3], (N_LAYERS_A, GM_WIDTH), 0.02)
    gm_w_s = nrm(ks[4], (N_LAYERS_A, GM_GROUPS, CHUNK, CHUNK), CHUNK ** -0.5)
    gm_b_s = 1.0 + nrm(ks[5], (N_LAYERS_A, GM_GROUPS, CHUNK), 0.02)
    gm_w_out = nrm(ks[6], (N_LAYERS_A, GM_WIDTH, D_MODEL), GM_WIDTH ** -0.5)

    qkv_w = (N_HEADS + 2 * N_KV_HEADS) * HEAD_DIM
    attn_w_qkv = nrm(ks[7], (N_LAYERS_B, D_MODEL, qkv_w), D_MODEL ** -0.5)
    attn_q_norm = 1.0 + nrm(ks[8], (N_LAYERS_B, HEAD_DIM), 0.02)
    attn_k_norm = 1.0 + nrm(ks[9], (N_LAYERS_B, HEAD_DIM), 0.02)
    attn_w_o = nrm(ks[10], (N_LAYERS_B, N_HEADS * HEAD_DIM, D_MODEL), (N_HEADS * HEAD_DIM) ** -0.5)

    ffn_w1 = nrm(ks[11], (DEPTH, D_MODEL, D_FF), D_MODEL ** -0.5)
    ffn_w2 = nrm(ks[12], (DEPTH, D_FF, D_MODEL), D_FF ** -0.5)

    norm_mix = 1.0 + nrm(ks[13], (DEPTH, D_MODEL), 0.02)
    norm_ffn = 1.0 + nrm(ks[14], (DEPTH, D_MODEL), 0.02)
    norm_final = 1.0 + nrm(ks[15], (D_MODEL,), 0.02)

    return {
        "x": x,
        "gm_w_in": gm_w_in, "gm_ln_g": gm_ln_g, "gm_ln_b": gm_ln_b,
        "gm_w_s": gm_w_s, "gm_b_s": gm_b_s, "gm_w_out": gm_w_out,
        "attn_w_qkv": attn_w_qkv, "attn_q_norm": attn_q_norm,
        "attn_k_norm": attn_k_norm, "attn_w_o": attn_w_o,
        "ffn_w1": ffn_w1, "ffn_w2": ffn_w2,
        "norm_mix": norm_mix, "norm_ffn": norm_ffn, "norm_final": norm_final,
    }


def reference(x, gm_w_in, gm_ln_g, gm_ln_b, gm_w_s, gm_b_s, gm_w_out,
              attn_w_qkv, attn_q_norm, attn_k_norm, attn_w_o,
              ffn_w1, ffn_w2, norm_mix, norm_ffn, norm_final):
    ia = 0
    ib = 0
    for i in range(DEPTH):
        h = rmsnorm(x, norm_mix[i])
        if i % N_MIXERS == 0:
            x = x + gmlp_mixer(h, gm_w_in[ia], gm_ln_g[ia], gm_ln_b[ia],
                               gm_w_s[ia], gm_b_s[ia], gm_w_out[ia])
            ia += 1
        else:
            x = x + attention_mixer(h, attn_w_qkv[ib], attn_q_norm[ib],
                                    attn_k_norm[ib], attn_w_o[ib])
            ib += 1
        h = rmsnorm(x, norm_ffn[i])
        x = x + sqrelu_mlp(h, ffn_w1[i], ffn_w2[i])
    return rmsnorm(x, norm_final)
```

```python
import math
from contextlib import ExitStack

import numpy as np
import concourse.bass as bass
import concourse.mybir as mybir
from concourse.bass_utils import run_bass_kernel_spmd

F32 = mybir.dt.float32
BF16 = mybir.dt.bfloat16
AF = mybir.ActivationFunctionType
ALU = mybir.AluOpType

D = 2048
S = 2048
T = 512
NT = S // T
KC = D // 128
DFF = 8192
EPS = 1e-6
NSB = 4
SLAB = 4096
SCALE = 128.0 ** -0.5

DBG = None


def slab_specs():
    A = []
    for nb in range(4):
        for kh in range(2):
            A.append(("tm", "gm_w_in", 0, 2048 + nb * 512, kh))
    for j in range(8):
        A.append(("fm", "gm_w_in", 0, j * 256))
    for j in range(8):
        A.append(("fm", "gm_w_out", 0, j * 256))
    for half in range(2):
        for j in range(16):
            A.append(("fm", "ffn_w1", 0, half * 4096 + j * 256))
        for n in range(16):
            A.append(("f2", "ffn_w2", 0, half, n))
    for j in range(8):
        A.append(("fm", "attn_w_qkv", 0, j * 256))
    for j in range(2):
        A.append(("fm", "attn_w_qkv", 0, 2048 + j * 256))
    for kh in range(2):
        A.append(("tm", "attn_w_qkv", 0, 2560, kh))
    B = []
    for j in range(8):
        B.append(("fm", "attn_w_o", 0, j * 256))
    for half in range(2):
        for j in range(16):
            B.append(("fm", "ffn_w1", 1, half * 4096 + j * 256))
        for n in range(16):
            B.append(("f2", "ffn_w2", 1, half, n))
    return A, B


def pack_slab(spec, W):
    kind = spec[0]
    w = W[spec[1]][spec[2]]
    if kind == "fm":
        c0 = spec[3]
        blk = w[:, c0:c0 + 256]
        return blk.reshape(16, 128, 256).transpose(1, 0, 2).reshape(128, SLAB)
    if kind == "tm":
        c0, kh = spec[3], spec[4]
        blk = w[kh * 1024:(kh + 1) * 1024, c0:c0 + 512]
        return blk.reshape(8, 128, 512).transpose(1, 0, 2).reshape(128, SLAB)
    half, n = spec[3], spec[4]
    blk = w[half * 4096:(half + 1) * 4096, n * 128:(n + 1) * 128]
    return blk.reshape(32, 128, 128).transpose(1, 0, 2).reshape(128, SLAB)


class Tok:
    __slots__ = ("sem", "val", "own")

    def __init__(self, sem, val, own):
        self.sem, self.val, self.own = sem, val, own


class Unit:
    __slots__ = ("w", "r", "name")

    def __init__(self, name):
        self.w = None
        self.r = {}
        self.name = name


class Eng:
    def __init__(self, name, e, sem, is_pe=False):
        self.name, self.e, self.sem, self.is_pe = name, e, sem, is_pe
        self.count = 0
        self.pending = []
        self.seen = {}


class Chan:
    def __init__(self, sem):
        self.sem = sem
        self.count = 0


class Prog:
    def __init__(self, nc, stack):
        self.nc = nc
        self.stack = stack
        self.nsem = 0
        self.pe = Eng("pe", nc.tensor, self.sem("pe"), True)
        self.act = Eng("act", nc.scalar, self.sem("act"))
        self.dve = Eng("dve", nc.vector, self.sem("dve"))
        self.pool = Eng("pool", nc.gpsimd, self.sem("pool"))
        self.sp = Eng("sp", nc.sync, self.sem("sp"))

    def sem(self, name):
        self.nsem += 1
        return self.stack.enter_context(self.nc.semaphore(f"s{self.nsem}_{name}"))

    def chan(self, name):
        return Chan(self.sem("ch_" + name))

    def _waits(self, eng, reads, writes):
        need = {}

        def add(tok, kind):
            if tok is None:
                return
            if tok.own is eng:
                if eng.is_pe or kind != "raw":
                    return
            assert tok.val is not None, f"dependency on unsignalled instruction ({eng.name})"
            k = id(tok.sem)
            if eng.seen.get(k, 0) >= tok.val:
                return
            if k not in need or need[k][1] < tok.val:
                need[k] = (tok.sem, tok.val)

        for u in reads:
            add(u.w, "raw")
        for u in writes:
            add(u.w, "waw")
            for t in u.r.values():
                add(t, "war")
        for sem, val in need.values():
            eng.e.wait_ge(sem, val)
            eng.seen[id(sem)] = val

    def op(self, eng, build, reads=(), writes=(), signal=True):
        self._waits(eng, reads, writes)
        ins = build()
        tok = Tok(eng.sem, None, eng)
        if signal:
            eng.count += 1
            ins.then_inc(eng.sem, 1)
            tok.val = eng.count
            for p in eng.pending:
                p.val = eng.count
            eng.pending = []
        else:
            eng.pending.append(tok)
        for u in reads:
            u.r[id(eng.sem)] = tok
        for u in writes:
            u.w = tok
            u.r = {}
        return tok

    def dma(self, q, ch, out, in_, reads=(), writes=()):
        self._waits(q, reads, writes)
        ins = q.e.dma_start(out=out, in_=in_)
        ch.count += 16
        ins.then_inc(ch.sem, 16)
        tok = Tok(ch.sem, ch.count, None)
        for u in reads:
            u.r[id(ch.sem)] = tok
        for u in writes:
            u.w = tok
            u.r = {}
        return tok


class Ring:
    def __init__(self, items):
        self.items = list(items)
        self.i = 0

    def next(self):
        v = self.items[self.i % len(self.items)]
        self.i += 1
        return v


def build_program(dbg=None):
    nc = bass.Bass("TRN2", target_bir_lowering=False)
    specA, specB = slab_specs()
    NA, NB_ = len(specA), len(specB)

    def din(name, shape, dt=F32):
        return nc.dram_tensor(name, shape, dt, kind="ExternalInput").ap()

    xT = din("xT", [128, NT, KC * T])
    wsl = din("wsl", [NA + NB_, 128, SLAB])
    gains_d = din("gains", [128, 5 * KC])
    lncol_d = din("lncol", [128, 2 * KC])
    wsT_d = din("wsT", [128, 16 * 128])
    bsb_d = din("bsb", [128, 16 * 128])
    qkg_d = din("qkg", [128, 4])
    pm_d = din("pm", [128, 128])
    rope_d = din("rope", [128, 2, S])
    yT = nc.dram_tensor("yT", [128, NT, KC * T], F32, kind="ExternalOutput").ap()
    Xs = nc.dram_tensor("Xs", [NT, 128, KC * T], F32).ap()
    Qs = nc.dram_tensor("Qs", [NT, 128, 16 * T], BF16).ap()
    Ks = nc.dram_tensor("Ks", [128, 4, S], BF16).ap()
    Vs = nc.dram_tensor("Vs", [128, 16, 512], BF16).ap()

    with ExitStack() as st:
        def sb(name, shape, dt):
            return st.enter_context(nc.sbuf_tensor(name, shape, dt))

        Xb = sb("Xb", [128, KC * T], F32)
        Hb = sb("Hb", [128, 2 * KC * T], BF16)
        R1 = sb("R1", [128, 32 * T], BF16)
        P2 = sb("P2", [128, 32 * T], BF16)
        SL = sb("SL", [128, NSB * SLAB], BF16)
        NTMP = 8
        TMP = sb("TMP", [128, NTMP * T], F32)
        NTB = 9
        TB = sb("TB", [128, NTB * T], BF16)
        ROPE = sb("ROPE", [128, 4 * T], F32)
        BIAS2 = sb("BIAS2", [128, 2048], F32)
        WST = sb("WST", [128, 2048], BF16)
        ONES = sb("ONES", [128, 128], BF16)
        PM = sb("PM", [128, 128], BF16)
        EPSC = sb("EPSC", [128, 1], F32)
        GAINS = sb("GAINS", [128, 5 * KC], F32)
        LNCOL = sb("LNCOL", [128, 2 * KC], F32)
        QKG = sb("QKG", [128, 4], F32)
        STATS = sb("STATS", [128, 24], F32)
        MV = sb("MV", [128, 2], F32)
        SM = sb("SM", [128, 2], F32)
        PS = st.enter_context(nc.psum_tensor("PS", [128, 8 * 512], F32))

        P = Prog(nc, st)
        pe, act, dve, pool, sp = P.pe, P.act, P.dve, P.pool, P.sp
        op, dma = P.op, P.dma

        def cols(t, u, n=1, w=T):
            return t[:, u * w:(u + n) * w]

        Xu = [Unit(f"X{i}") for i in range(KC)]
        Hu = [[Unit(f"H{j}_{i}") for i in range(KC)] for j in range(2)]
        R1u = [Unit(f"R1_{i}") for i in range(32)]
        P2u = [Unit(f"P2_{i}") for i in range(32)]
        SLu = [Unit(f"SL{i}") for i in range(NSB)]
        TMPu = [Unit(f"TMP{i}") for i in range(NTMP)]
        TBu = [Unit(f"TB{i}") for i in range(NTB)]
        ROPEu = [Unit(f"ROPE{i}") for i in range(4)]
        PSu = [Unit(f"PS{i}") for i in range(8)]
        u_bias2, u_wst, u_ones, u_pm, u_negh = (Unit(n) for n in ("bias2", "wst", "ones", "pm", "negh"))
        u_gains, u_lncol, u_qkg, u_stats, u_mv, u_sm = (Unit(n) for n in ("gains", "lncol", "qkg", "stats", "mv", "sm"))
        Xs_u = [Unit(f"Xs{i}") for i in range(NT)]
        Qs_u = [Unit(f"Qs{i}") for i in range(NT)]
        Ks_u = [Unit(f"Ks{i}") for i in range(NT)]
        Vs_u = [Unit(f"Vs{i}") for i in range(NT)]

        def bank(b, lo=0, n=512):
            return PS[:, b * 512 + lo:b * 512 + lo + n]

        mm = Ring([0, 1, 2, 3])
        aux = Ring([4, 5, 6, 7])
        sring = Ring([0, 1, 2])
        oring = Ring([3, 4])
        dring = Ring([5, 6])
        tmp = Ring([3, 4, 5])
        q32r = Ring([0, 1, 2])
        rsv = Ring([NTMP - 2, NTMP - 1])
        tb = Ring(range(NTB))
        hlr = Ring([(3, 4), (5, 6), (7, 8)])
        sqr = Ring([0, 1, 2])

        slab_order = []
        for tt in range(NT):
            slab_order += list(range(NA))
        for tt in range(NT):
            slab_order += list(range(NA, NA + NB_))
        spec_all = specA + specB
        sl_ch = [P.chan(f"sl{i}") for i in range(NSB)]
        sl_state = {"pos": 0, "loaded": 0}

        def get_slab(kind):
            i = sl_state["pos"]
            sl_state["pos"] += 1
            assert spec_all[slab_order[i]][0] == kind, (i, spec_all[slab_order[i]], kind)
            while sl_state["loaded"] < min(len(slab_order), i + NSB):
                j = sl_state["loaded"]
                bi = j % NSB
                dma(pool, sl_ch[bi], cols(SL, bi, 1, SLAB), wsl[slab_order[j]], reads=(), writes=[SLu[bi]])
                sl_state["loaded"] += 1
            bi = i % NSB
            return cols(SL, bi, 1, SLAB), SLu[bi]

        ch_c = [P.chan(f"c{i}") for i in range(6)]
        dma(sp, ch_c[0], GAINS[:], gains_d, writes=[u_gains])
        dma(sp, ch_c[1], LNCOL[:], lncol_d, writes=[u_lncol])
        dma(sp, ch_c[2], QKG[:], qkg_d, writes=[u_qkg])
        dma(pool, ch_c[3], PM[:], pm_d, writes=[u_pm])
        dma(sp, ch_c[4], BIAS2[:], bsb_d, writes=[u_bias2])
        dma(pool, ch_c[5], WST[:], wsT_d, writes=[u_wst])
        op(dve, lambda: nc.vector.memset(ONES[:], 1.0), writes=[u_ones])
        op(dve, lambda: nc.vector.memset(EPSC[:], EPS), writes=[u_negh])
        for gq in range(4):
            b = aux.next()
            op(pe, lambda: nc.tensor.matmul(bank(b), lhsT=ONES[:], rhs=WST[:, gq * 512:(gq + 1) * 512], start=True, stop=True),
               reads=[u_ones, u_wst], writes=[PSu[b]])
            for j in range(4):
                g = gq * 4 + j
                op(dve, lambda: nc.vector.scalar_tensor_tensor(
                    out=BIAS2[:, g * 128:(g + 1) * 128], in0=bank(b, j * 128, 128), scalar=LNCOL[:, KC + g:KC + g + 1],
                    in1=BIAS2[:, g * 128:(g + 1) * 128], op0=ALU.mult, op1=ALU.add),
                   reads=[PSu[b], u_lncol, u_bias2], writes=[u_bias2])

        bg = []

        def inject(n=1):
            for _ in range(n):
                if bg:
                    bg.pop(0)()

        def flush_bg():
            while bg:
                bg.pop(0)()

        hsel = Ring([0, 1])

        def hcols(hb, kc, n=1):
            return Hb[:, (hb * KC + kc) * T:(hb * KC + kc + n) * T]

        def rstd_from(ss, scale, r=None):
            t = tmp.next()
            op(act, lambda: nc.scalar.activation(out=cols(TMP, t), in_=bank(ss), func=AF.Ln, bias=EPSC[:, 0:1], scale=scale),
               reads=[PSu[ss], u_negh], writes=[TMPu[t]])
            if r is None:
                r = tmp.next()
            op(act, lambda: nc.scalar.activation(out=cols(TMP, r), in_=cols(TMP, t), func=AF.Exp, scale=-0.5),
               reads=[TMPu[t]], writes=[TMPu[r]])
            return r

        def rmsnorm_steps(gidx, to_x=False):
            hb = hsel.next()
            hu = Hu[hb]
            state = {}
            steps = []

            def sq_step(q4):
                def f():
                    op(act, lambda: nc.scalar.activation(out=hcols(hb, q4 * 4, 4), in_=cols(Xb, q4 * 4, 4), func=AF.Square),
                       reads=Xu[q4 * 4:q4 * 4 + 4], writes=hu[q4 * 4:q4 * 4 + 4])
                return f

            def sum_step():
                ss = aux.next()
                state["ss"] = ss
                for kc in range(KC):
                    op(pe, lambda: nc.tensor.matmul(bank(ss), lhsT=ONES[:], rhs=hcols(hb, kc), start=(kc == 0), stop=(kc == KC - 1)),
                       reads=[hu[kc], u_ones], writes=[PSu[ss]], signal=(kc == KC - 1))

            def rstd_step():
                state["r"] = rstd_from(state["ss"], 1.0 / D, rsv.next())

            def norm_step(q4):
                def f():
                    r = state["r"]
                    for kc in range(q4 * 4, q4 * 4 + 4):
                        dst, du = (cols(Xb, kc), Xu[kc]) if to_x else (hcols(hb, kc), hu[kc])
                        op(dve, lambda: nc.vector.scalar_tensor_tensor(
                            out=dst, in0=cols(Xb, kc), scalar=GAINS[:, gidx * KC + kc:gidx * KC + kc + 1], in1=cols(TMP, r),
                            op0=ALU.mult, op1=ALU.mult),
                           reads=[Xu[kc], TMPu[r], u_gains], writes=[du])
                return f

            def sum_rstd_step():
                sum_step()
                rstd_step()

            steps += [sq_step(q) for q in range(4)]
            steps += [sum_rstd_step]
            steps += [norm_step(q) for q in range(4)]
            return hb, steps

        def rmsnorm(gidx, to_x=False):
            hb, steps = rmsnorm_steps(gidx, to_x)
            for f in steps:
                f()
            return hb

        def make_acc(gidx, to_x=False):
            hb = hsel.next()
            hu = Hu[hb]
            st_ = {"pend": None, "ss": aux.next(), "n": 0}

            def mm_(n, last):
                ss = st_["ss"]
                op(pe, lambda: nc.tensor.matmul(bank(ss), lhsT=ONES[:], rhs=hcols(hb, n), start=(st_["n"] == 0), stop=last),
                   reads=[hu[n], u_ones], writes=[PSu[ss]], signal=last)
                st_["n"] += 1

            def on_chunk(n):
                op(act, lambda: nc.scalar.activation(out=hcols(hb, n), in_=cols(Xb, n), func=AF.Square),
                   reads=[Xu[n]], writes=[hu[n]])
                if st_["pend"] is not None:
                    mm_(st_["pend"], False)
                st_["pend"] = n

            def tail_steps():
                state = {}

                def first():
                    mm_(st_["pend"], True)
                    assert st_["n"] == KC
                    state["r"] = rstd_from(st_["ss"], 1.0 / D, rsv.next())

                def norm_step(q4):
                    def f():
                        r = state["r"]
                        for kc in range(q4 * 4, q4 * 4 + 4):
                            dst, du = (cols(Xb, kc), Xu[kc]) if to_x else (hcols(hb, kc), hu[kc])
                            op(dve, lambda: nc.vector.scalar_tensor_tensor(
                                out=dst, in0=cols(Xb, kc), scalar=GAINS[:, gidx * KC + kc:gidx * KC + kc + 1], in1=cols(TMP, r),
                                op0=ALU.mult, op1=ALU.mult),
                               reads=[Xu[kc], TMPu[r], u_gains], writes=[du])
                    return f
                return [first] + [norm_step(q) for q in range(4)]

            def finish():
                for f in tail_steps():
                    f()
                return hb

            def evac(n, b):
                resid_evac(n, b)
                on_chunk(n)
            return {"evac": evac, "finish": finish, "tail_steps": tail_steps, "hb": hb}

        def fm_linear(nslabs, rhs_fn, evac):
            for j in range(nslabs):
                sl, slu = get_slab("fm")
                for c in range(2):
                    n = 2 * j + c
                    b = mm.next()
                    for kc in range(KC):
                        rhs, ru = rhs_fn(kc)
                        last = kc == KC - 1
                        op(pe, lambda: nc.tensor.matmul(bank(b), lhsT=sl[:, kc * 256 + c * 128:kc * 256 + c * 128 + 128], rhs=rhs,
                                                        start=(kc == 0), stop=last),
                           reads=[slu, ru], writes=[PSu[b]], signal=last)
                    evac(n, b)
                    inject()

        def tm_linear(n_nb, hb, evac):
            for nb in range(n_nb):
                banks = [mm.next() for _ in range(4)]
                for kh in range(2):
                    sl, slu = get_slab("tm")
                    for tc in range(4):
                        for k8 in range(8):
                            kc = kh * 8 + k8
                            last = kc == KC - 1
                            sig = last or (tc == 3 and k8 == 7)
                            op(pe, lambda: nc.tensor.matmul(bank(banks[tc]), lhsT=hcols(hb, kc)[:, tc * 128:(tc + 1) * 128],
                                                            rhs=sl[:, k8 * 512:(k8 + 1) * 512], start=(kc == 0), stop=last),
                               reads=[slu, Hu[hb][kc]], writes=[PSu[banks[tc]]], signal=sig)
                        inject()
                for tc in range(4):
                    evac(nb, tc, banks[tc])

        def h_rhs(hb):
            return lambda kc: (hcols(hb, kc), Hu[hb][kc])

        def resid_evac(n, b):
            op(dve, lambda: nc.vector.tensor_tensor(out=cols(Xb, n), in0=bank(b), in1=cols(Xb, n), op=ALU.add),
               reads=[PSu[b], Xu[n]], writes=[Xu[n]])

        def ffn(hb, next_gidx, next_to_x=False):
            acc = None
            for half in range(2):
                def ev1(n, b):
                    r = tmp.next()
                    op(act, lambda: nc.scalar.activation(out=cols(TMP, r), in_=bank(b), func=AF.Relu),
                       reads=[PSu[b]], writes=[TMPu[r]])
                    op(dve, lambda: nc.vector.tensor_tensor(out=cols(R1, n), in0=cols(TMP, r), in1=cols(TMP, r), op=ALU.mult),
                       reads=[TMPu[r]], writes=[R1u[n]])
                fm_linear(16, h_rhs(hb), ev1)
                for n in range(16):
                    sl, slu = get_slab("f2")
                    b = mm.next()
                    for kc in range(32):
                        op(pe, lambda: nc.tensor.matmul(bank(b), lhsT=sl[:, kc * 128:(kc + 1) * 128], rhs=cols(R1, kc),
                                                        start=(kc == 0), stop=(kc == 31)),
                           reads=[slu, R1u[kc]], writes=[PSu[b]], signal=(kc == 31))
                    if half == 1:
                        if acc is None:
                            acc = make_acc(next_gidx, next_to_x)
                        acc["evac"](n, b)
                    else:
                        resid_evac(n, b)
                    inject()
            return acc

        def dump_x(tt):
            ch = P.chan("dbg")
            dma(sp, ch, yT[:, tt], Xb[:], reads=Xu, writes=[])
            return ch

        out_ch = P.chan("out")
        chX = P.chan("X")
        chR = [P.chan(f"rope{i}") for i in range(4)]
        chQ, chK, chV = P.chan("Q"), P.chan("K"), P.chan("V")
        dbg_ch = []

        def load_x_A(tt):
            dma(sp, chX, Xb[:], xT[:, tt], writes=Xu)

        ntA = NT if dbg is None else 1
        hb_next = None
        for tt in range(ntA):
            if tt == 0:
                load_x_A(0)
                hb = rmsnorm(0)
            else:
                hb = hb_next

            def v_evac(nb, tc, b):
                v32 = R1[:, tc * 4096:(tc + 1) * 4096].bitcast(F32)
                op(act, lambda: nc.scalar.activation(out=v32[:, nb * 512:(nb + 1) * 512], in_=bank(b), func=AF.Gelu),
                   reads=[PSu[b]], writes=[R1u[tc * 8 + nb * 2], R1u[tc * 8 + nb * 2 + 1]])
            tm_linear(4, hb, v_evac)

            def ln_step(tc):
                def f():
                    v32 = R1[:, tc * 4096:(tc + 1) * 4096].bitcast(F32)
                    for nb in range(4):
                        op(dve, lambda: nc.vector.bn_stats(out=STATS[:, nb * 6:(nb + 1) * 6], in_=v32[:, nb * 512:(nb + 1) * 512]),
                           reads=[R1u[tc * 8 + nb * 2], R1u[tc * 8 + nb * 2 + 1]], writes=[u_stats])
                    op(dve, lambda: nc.vector.bn_aggr(out=MV[:], in_=STATS[:]), reads=[u_stats], writes=[u_mv])
                    op(act, lambda: nc.scalar.activation(out=SM[:, 0:1], in_=MV[:, 1:2], func=AF.Ln, bias=EPSC[:, 0:1], scale=1.0),
                       reads=[u_mv, u_negh], writes=[u_sm])
                    op(act, lambda: nc.scalar.activation(out=SM[:, 1:2], in_=SM[:, 0:1], func=AF.Exp, scale=-0.5),
                       reads=[u_sm], writes=[u_sm])
                    op(dve, lambda: nc.vector.tensor_scalar(out=cols(P2, 16 + 4 * tc, 4), in0=v32, scalar1=MV[:, 0:1], scalar2=SM[:, 1:2],
                                                            op0=ALU.subtract, op1=ALU.mult),
                       reads=R1u[tc * 8:tc * 8 + 8] + [u_mv, u_sm], writes=P2u[16 + 4 * tc:16 + 4 * tc + 4])
                return f
            for tc in range(4):
                bg.append(ln_step(tc))
                bg.append(lambda: None)
            fm_linear(8, h_rhs(hb),
                      lambda n, b: op(act, lambda: nc.scalar.activation(out=cols(P2, n), in_=bank(b), func=AF.Gelu),
                                      reads=[PSu[b]], writes=[P2u[n]]))
            flush_bg()
            for tc in range(4):
                vn = cols(P2, 16 + 4 * tc, 4)
                for gq in range(4):
                    b = aux.next()
                    for j in range(4):
                        g = gq * 4 + j
                        op(pe, lambda: nc.tensor.matmul(bank(b, j * 128, 128), lhsT=vn[:, g * 128:(g + 1) * 128],
                                                        rhs=WST[:, g * 128:(g + 1) * 128], start=True, stop=True),
                           reads=[P2u[16 + 4 * tc + gq], u_wst], writes=[PSu[b]], signal=(j == 3))
                    t = tmp.next()
                    for j in range(4):
                        g = gq * 4 + j
                        op(dve, lambda: nc.vector.scalar_tensor_tensor(
                            out=TMP[:, t * T + j * 128:t * T + (j + 1) * 128], in0=bank(b, j * 128, 128), scalar=LNCOL[:, g:g + 1],
                            in1=BIAS2[:, g * 128:(g + 1) * 128], op0=ALU.mult, op1=ALU.add),
                           reads=[PSu[b], u_lncol, u_bias2], writes=[TMPu[t]])
                    uview = cols(P2, gq * 4, 4).rearrange("p (g t) -> p g t", t=T)[:, :, tc * 128:(tc + 1) * 128]
                    tview = cols(TMP, t).rearrange("p (g t) -> p g t", t=128)
                    op(dve, lambda: nc.vector.tensor_tensor(out=uview, in0=tview, in1=uview, op=ALU.mult),
                       reads=[TMPu[t]] + P2u[gq * 4:gq * 4 + 4], writes=P2u[gq * 4:gq * 4 + 4])
            acc1 = make_acc(1)
            fm_linear(8, lambda kc: (cols(P2, kc), P2u[kc]), acc1["evac"])
            if dbg == "mix0":
                dbg_ch.append(dump_x(tt))
                break
            hb1 = acc1["finish"]()

            def rope_load(tt=tt):
                for i in range(4):
                    dma(sp, chR[i], cols(ROPE, i), rope_d[:, i % 2, tt * T:(tt + 1) * T], writes=[ROPEu[i]])

            def rope_scale():
                for i in range(4):
                    op(dve, lambda: nc.vector.tensor_scalar(out=cols(ROPE, i), in0=cols(ROPE, i), scalar1=QKG[:, i:i + 1], scalar2=None,
                                                            op0=ALU.mult),
                       reads=[ROPEu[i], u_qkg], writes=[ROPEu[i]])
            bg.append(rope_load)
            bg.extend([lambda: None] * 8)
            bg.append(rope_scale)
            acc2 = ffn(hb1, 2)
            flush_bg()
            dma(sp, chX, Xs[tt], Xb[:], reads=Xu, writes=[Xs_u[tt]])
            if dbg == "ffn0":
                dbg_ch.append(dump_x(tt))
                break

            hb2 = acc2["finish"]()
            if tt + 1 < ntA:
                hb_next, nsteps = rmsnorm_steps(0)
                bg.append(lambda: load_x_A(tt + 1))
                bg.extend([lambda: None] * 5)
                bg.extend(nsteps)
            pend = []

            def qk_post(kind):
                ci = 0 if kind == "q" else 2

                def ev(n, b):
                    outu = n if kind == "q" else 16 + n
                    q32 = q32r.next()
                    op(act, lambda: nc.scalar.activation(out=cols(TMP, q32), in_=bank(b), func=AF.Copy),
                       reads=[PSu[b]], writes=[TMPu[q32]])
                    sq = sqr.next()
                    op(act, lambda: nc.scalar.activation(out=cols(TB, sq), in_=bank(b), func=AF.Square),
                       reads=[PSu[b]], writes=[TBu[sq]])
                    hi, lo = hlr.next()
                    op(act, lambda: nc.scalar.activation(out=cols(TB, hi), in_=bank(b), func=AF.Copy),
                       reads=[PSu[b]], writes=[TBu[hi]])
                    op(pool, lambda: nc.gpsimd.tensor_tensor(out=cols(TB, lo), in0=cols(TMP, q32), in1=cols(TB, hi), op=ALU.subtract),
                       reads=[TMPu[q32], TBu[hi]], writes=[TBu[lo]])

                    def stage2():
                        ss = aux.next()
                        op(pe, lambda: nc.tensor.matmul(bank(ss), lhsT=ONES[:], rhs=cols(TB, sq), start=True, stop=True),
                           reads=[TBu[sq], u_ones], writes=[PSu[ss]])
                        pp = aux.next()
                        op(pe, lambda: nc.tensor.matmul(bank(pp), lhsT=PM[:], rhs=cols(TB, hi), start=True, stop=False),
                           reads=[TBu[hi], u_pm], writes=[PSu[pp]], signal=False)
                        op(pe, lambda: nc.tensor.matmul(bank(pp), lhsT=PM[:], rhs=cols(TB, lo), start=False, stop=True),
                           reads=[TBu[lo], u_pm], writes=[PSu[pp]])
                        r = rstd_from(ss, 1.0 / 128)
                        t = tmp.next()
                        a = tmp.next()
                        op(dve, lambda: nc.vector.tensor_tensor(out=cols(TMP, a), in0=cols(TMP, q32), in1=cols(ROPE, ci), op=ALU.mult),
                           reads=[TMPu[q32], ROPEu[ci]], writes=[TMPu[a]])
                        op(dve, lambda: nc.vector.tensor_tensor(out=cols(TMP, t), in0=bank(pp), in1=cols(ROPE, ci + 1), op=ALU.mult),
                           reads=[PSu[pp], ROPEu[ci + 1], TMPu[t]], writes=[TMPu[t]])
                        op(dve, lambda: nc.vector.tensor_tensor(out=cols(TMP, a), in0=cols(TMP, a), in1=cols(TMP, t), op=ALU.add),
                           reads=[TMPu[a], TMPu[t]], writes=[TMPu[a]])
                        op(dve, lambda: nc.vector.tensor_tensor(out=cols(R1, outu), in0=cols(TMP, a), in1=cols(TMP, r), op=ALU.mult),
                           reads=[TMPu[a], TMPu[r]], writes=[R1u[outu]])
                    if pend:
                        pend.pop()()
                    pend.append(stage2)
                return ev

            fm_linear(8, h_rhs(hb2), qk_post("q"))
            fm_linear(2, h_rhs(hb2), qk_post("k"))
            if pend:
                pend.pop()()
            tm_linear(1, hb2, lambda nb, tc, b: op(act, lambda: nc.scalar.activation(out=cols(R1, 20 + tc), in_=bank(b), func=AF.Copy),
                                                   reads=[PSu[b]], writes=[R1u[20 + tc]]))
            flush_bg()
            dma(sp, chQ, Qs[tt], cols(R1, 0, 16), reads=R1u[0:16], writes=[Qs_u[tt]])
            dma(sp, chK, Ks[:, :, tt * T:(tt + 1) * T], cols(R1, 16, 4).rearrange("p (h t) -> p h t", t=T),
                reads=R1u[16:20], writes=[Ks_u[tt]])
            dma(sp, chV, Vs[:, tt * 4:(tt + 1) * 4, :], cols(R1, 20, 4).rearrange("p (c f) -> p c f", f=512),
                reads=R1u[20:24], writes=[Vs_u[tt]])

        ntB = NT if dbg is None else 0
        if ntB:
            chKV = [P.chan("KVa"), P.chan("KVb")]
            dma(sp, chKV[0], cols(P2, 0, 16), Ks.rearrange("p h t -> p (h t)"), reads=Ks_u, writes=P2u[0:16])
            dma(sp, chKV[1], cols(P2, 16, 16), Vs.rearrange("p c f -> p (c f)"), reads=Vs_u, writes=P2u[16:32])
            dma(sp, chQ, cols(R1, 0, 16), Qs[0], reads=[Qs_u[0]], writes=R1u[0:16])
            dma(sp, chX, Xb[:], Xs[0], reads=[Xs_u[0]], writes=Xu)
        for tt in range(ntB):
            NG = 16 * 8

            def s_pair(g):
                h, j = divmod(g, 8)
                kvh = h // 4
                sb0 = 2 * (g % 2)
                for i in range(2):
                    kc = 2 * j + i
                    op(pe, lambda: nc.tensor.matmul(bank(sb0 + i), lhsT=P2[:, kvh * S + kc * 128:kvh * S + (kc + 1) * 128], rhs=cols(R1, h),
                                                    start=True, stop=True),
                       reads=[P2u[kvh * 4 + kc // 4], R1u[h]], writes=[PSu[sb0 + i]], signal=(i == 1))

            def exp_pair(g):
                sb0 = 2 * (g % 2)
                p0 = 2 * (g % 3)
                op(act, lambda: nc.scalar.activation(out=cols(TB, p0, 2), in_=PS[:, sb0 * 512:(sb0 + 2) * 512], func=AF.Exp, scale=SCALE),
                   reads=[PSu[sb0], PSu[sb0 + 1]], writes=[TBu[p0], TBu[p0 + 1]])

            def od_pair(g):
                h, j = divmod(g, 8)
                kvh = h // 4
                p0 = 2 * (g % 3)
                ob, db = 4 + h % 2, 6 + h % 2
                for i in range(2):
                    kc = 2 * j + i
                    op(pe, lambda: nc.tensor.matmul(bank(ob), lhsT=P2[:, (16 + kc) * T + kvh * 128:(16 + kc) * T + (kvh + 1) * 128],
                                                    rhs=cols(TB, p0 + i), start=(kc == 0), stop=(kc == 15)),
                       reads=[P2u[16 + kc], TBu[p0 + i]], writes=[PSu[ob]], signal=(kc == 15))
                    op(pe, lambda: nc.tensor.matmul(bank(db), lhsT=ONES[:], rhs=cols(TB, p0 + i), start=(kc == 0), stop=(kc == 15)),
                       reads=[TBu[p0 + i], u_ones], writes=[PSu[db]], signal=(i == 1))
                if j == 7:
                    rd = tmp.next()
                    op(dve, lambda: nc.vector.reciprocal(out=cols(TMP, rd), in_=bank(db)), reads=[PSu[db]], writes=[TMPu[rd]])
                    op(dve, lambda: nc.vector.tensor_tensor(out=cols(R1, 16 + h), in0=bank(ob), in1=cols(TMP, rd), op=ALU.mult),
                       reads=[PSu[ob], TMPu[rd]], writes=[R1u[16 + h]])
                    inject()

            s_pair(0)
            for g in range(NG + 1):
                if g < NG:
                    exp_pair(g)
                if g + 1 < NG:
                    s_pair(g + 1)
                if g >= 1:
                    od_pair(g - 1)
            flush_bg()
            acc3 = make_acc(3)
            fm_linear(8, lambda kc: (cols(R1, 16 + kc), R1u[16 + kc]), acc3["evac"])
            hb3 = acc3["finish"]()
            acc4 = ffn(hb3, 4, True)
            last = tt + 1 == ntB
            if not last:
                dma(sp, chQ, cols(R1, 0, 16), Qs[tt + 1], reads=[Qs_u[tt + 1]], writes=R1u[0:16])
            fsteps = acc4["tail_steps"]()
            fsteps.pop(0)()

            def store_step(tt=tt):
                dma(sp, out_ch, yT[:, tt], Xb[:], reads=Xu, writes=[])

            def xload_step(tt=tt):
                dma(sp, chX, Xb[:], Xs[tt + 1], reads=[Xs_u[tt + 1]], writes=Xu)
            fsteps.append(store_step)
            if not last:
                fsteps.append(xload_step)
                bg.extend(fsteps)
            else:
                for f in fsteps:
                    f()

        for ch in [out_ch] + dbg_ch:
            if ch.count:
                sp.e.wait_ge(ch.sem, ch.count)
        for e in (pe, act, dve, pool):
            if e.count:
                sp.e.wait_ge(e.sem, e.count)
    return nc


def rope_tables():
    half = 32
    inv_freq = (10000.0 ** (-np.arange(0, 64, 2, dtype=np.float32) / np.float32(64))).astype(np.float32)
    t = np.arange(S)
    row = (t // 64).astype(np.float32)
    col = (t % 64).astype(np.float32)
    cos = np.zeros((128, S), np.float32)
    sin = np.zeros((128, S), np.float32)
    for d in range(128):
        pos = row if d < 64 else col
        j = d % 64
        ang = (pos * inv_freq[j % half]).astype(np.float32)
        cos[d] = np.cos(ang)
        sgn = -1.0 if j < half else 1.0
        sin[d] = sgn * np.sin(ang)
    return np.stack([cos, sin], axis=1).astype(np.float32)


def partner_perm():
    p = np.arange(128)
    j = p % 64
    return np.where(j < 32, p + 32, p - 32)


def prep_shared(inputs):
    W = {k: np.asarray(v, dtype=np.float32) for k, v in inputs.items() if k != "x"}
    specA, specB = slab_specs()
    wsl = np.empty((len(specA) + len(specB), 128, SLAB), np.float32)
    for i, sp_ in enumerate(specA + specB):
        wsl[i] = pack_slab(sp_, W)

    def colmajor(v):
        return v.reshape(KC, 128).T

    gains = np.concatenate([colmajor(W["norm_mix"][0]), colmajor(W["norm_ffn"][0]), colmajor(W["norm_mix"][1]),
                            colmajor(W["norm_ffn"][1]), colmajor(W["norm_final"])], axis=1)
    lncol = np.concatenate([colmajor(W["gm_ln_g"][0]), colmajor(W["gm_ln_b"][0])], axis=1)
    wsT = W["gm_w_s"][0].transpose(2, 0, 1).reshape(128, 16 * 128)
    bsb = np.broadcast_to(W["gm_b_s"][0].reshape(1, 16 * 128), (128, 16 * 128))
    perm = partner_perm()
    gq, gk = W["attn_q_norm"][0], W["attn_k_norm"][0]
    qkg = np.stack([gq, gq[perm], gk, gk[perm]], axis=1)
    pm = np.zeros((128, 128), np.float32)
    pm[perm, np.arange(128)] = 1.0
    c = np.ascontiguousarray
    return {"wsl": wsl, "gains": c(gains, np.float32), "lncol": c(lncol, np.float32), "wsT": c(wsT, np.float32),
            "bsb": c(bsb, np.float32), "qkg": c(qkg, np.float32), "pm": pm, "rope": rope_tables()}


def to_fm(xb):
    return np.ascontiguousarray(xb.reshape(NT, T, KC, 128).transpose(3, 0, 2, 1).reshape(128, NT, KC * T))


def from_fm(y):
    return np.ascontiguousarray(y.reshape(128, NT, KC, T).transpose(1, 3, 2, 0).reshape(S, D))


def kernel(**inputs):
    x = np.asarray(inputs["x"], dtype=np.float32)
    shared = prep_shared(inputs)
    nc = build_program(DBG)
    in_maps = []
    for b in range(8):
        m = dict(shared)
        m["xT"] = to_fm(x[b])
        in_maps.append(m)
    res = run_bass_kernel_spmd(nc, in_maps, core_ids=list(range(8)))
    out = np.stack([from_fm(np.asarray(res.results[b]["yT"], dtype=np.float32)) for b in range(8)], axis=0)
    return out.astype(np.float32)
```

```python
import math
from contextlib import ExitStack

import numpy as np
import concourse.bass as bass
import concourse.mybir as mybir
from concourse.bass_utils import run_bass_kernel_spmd

F32 = mybir.dt.float32
BF16 = mybir.dt.bfloat16
AF = mybir.ActivationFunctionType
ALU = mybir.AluOpType

D = 2048
S = 2048
T = 512
NT = S // T
KC = D // 128
DFF = 8192
EPS = 1e-6
NSB = 4
SLAB = 4096
SCALE = 128.0 ** -0.5

DBG = None


def slab_specs():
    A = []
    for nb in range(4):
        for kh in range(2):
            A.append(("tm", "gm_w_in", 0, 2048 + nb * 512, kh))
    for j in range(8):
        A.append(("fm", "gm_w_in", 0, j * 256))
    for j in range(8):
        A.append(("fm", "gm_w_out", 0, j * 256))
    for half in range(2):
        for j in range(16):
            A.append(("fm", "ffn_w1", 0, half * 4096 + j * 256))
        for n in range(16):
            A.append(("f2", "ffn_w2", 0, half, n))
    for j in range(8):
        A.append(("fm", "attn_w_qkv", 0, j * 256))
    for j in range(2):
        A.append(("fm", "attn_w_qkv", 0, 2048 + j * 256))
    for kh in range(2):
        A.append(("tm", "attn_w_qkv", 0, 2560, kh))
    B = []
    for j in range(8):
        B.append(("fm", "attn_w_o", 0, j * 256))
    for half in range(2):
        for j in range(16):
            B.append(("fm", "ffn_w1", 1, half * 4096 + j * 256))
        for n in range(16):
            B.append(("f2", "ffn_w2", 1, half, n))
    return A, B


def pack_slab(spec, W):
    kind = spec[0]
    w = W[spec[1]][spec[2]]
    if kind == "fm":
        c0 = spec[3]
        blk = w[:, c0:c0 + 256]
        return blk.reshape(16, 128, 256).transpose(1, 0, 2).reshape(128, SLAB)
    if kind == "tm":
        c0, kh = spec[3], spec[4]
        blk = w[kh * 1024:(kh + 1) * 1024, c0:c0 + 512]
        return blk.reshape(8, 128, 512).transpose(1, 0, 2).reshape(128, SLAB)
    half, n = spec[3], spec[4]
    blk = w[half * 4096:(half + 1) * 4096, n * 128:(n + 1) * 128]
    return blk.reshape(32, 128, 128).transpose(1, 0, 2).reshape(128, SLAB)


class Tok:
    __slots__ = ("sem", "val", "own")

    def __init__(self, sem, val, own):
        self.sem, self.val, self.own = sem, val, own


class Unit:
    __slots__ = ("w", "r", "name")

    def __init__(self, name):
        self.w = None
        self.r = {}
        self.name = name


class Eng:
    def __init__(self, name, e, sem, is_pe=False):
        self.name, self.e, self.sem, self.is_pe = name, e, sem, is_pe
        self.count = 0
        self.pending = []
        self.seen = {}


class Chan:
    def __init__(self, sem):
        self.sem = sem
        self.count = 0


class Prog:
    def __init__(self, nc, stack):
        self.nc = nc
        self.stack = stack
        self.nsem = 0
        self.pe = Eng("pe", nc.tensor, self.sem("pe"), True)
        self.act = Eng("act", nc.scalar, self.sem("act"))
        self.dve = Eng("dve", nc.vector, self.sem("dve"))
        self.pool = Eng("pool", nc.gpsimd, self.sem("pool"))
        self.sp = Eng("sp", nc.sync, self.sem("sp"))

    def sem(self, name):
        self.nsem += 1
        return self.stack.enter_context(self.nc.semaphore(f"s{self.nsem}_{name}"))

    def chan(self, name):
        return Chan(self.sem("ch_" + name))

    def _waits(self, eng, reads, writes):
        need = {}

        def add(tok, kind):
            if tok is None:
                return
            if tok.own is eng:
                if eng.is_pe or kind != "raw":
                    return
            assert tok.val is not None, f"dependency on unsignalled instruction ({eng.name})"
            k = id(tok.sem)
            if eng.seen.get(k, 0) >= tok.val:
                return
            if k not in need or need[k][1] < tok.val:
                need[k] = (tok.sem, tok.val)

        for u in reads:
            add(u.w, "raw")
        for u in writes:
            add(u.w, "waw")
            for t in u.r.values():
                add(t, "war")
        for sem, val in need.values():
            eng.e.wait_ge(sem, val)
            eng.seen[id(sem)] = val

    def op(self, eng, build, reads=(), writes=(), signal=True):
        self._waits(eng, reads, writes)
        ins = build()
        tok = Tok(eng.sem, None, eng)
        if signal:
            eng.count += 1
            ins.then_inc(eng.sem, 1)
            tok.val = eng.count
            for p in eng.pending:
                p.val = eng.count
            eng.pending = []
        else:
            eng.pending.append(tok)
        for u in reads:
            u.r[id(eng.sem)] = tok
        for u in writes:
            u.w = tok
            u.r = {}
        return tok

    def dma(self, q, ch, out, in_, reads=(), writes=()):
        self._waits(q, reads, writes)
        ins = q.e.dma_start(out=out, in_=in_)
        ch.count += 16
        ins.then_inc(ch.sem, 16)
        tok = Tok(ch.sem, ch.count, None)
        for u in reads:
            u.r[id(ch.sem)] = tok
        for u in writes:
            u.w = tok
            u.r = {}
        return tok


class Ring:
    def __init__(self, items):
        self.items = list(items)
        self.i = 0

    def next(self):
        v = self.items[self.i % len(self.items)]
        self.i += 1
        return v


def build_program(dbg=None):
    nc = bass.Bass("TRN2", target_bir_lowering=False)
    specA, specB = slab_specs()
    NA, NB_ = len(specA), len(specB)

    def din(name, shape, dt=F32):
        return nc.dram_tensor(name, shape, dt, kind="ExternalInput").ap()

    xT = din("xT", [128, NT, KC * T])
    wsl = din("wsl", [NA + NB_, 128, SLAB])
    gains_d = din("gains", [128, 5 * KC])
    lncol_d = din("lncol", [128, 2 * KC])
    wsT_d = din("wsT", [128, 16 * 128])
    bsb_d = din("bsb", [128, 16 * 128])
    qkg_d = din("qkg", [128, 4])
    pm_d = din("pm", [128, 128])
    rope_d = din("rope", [128, 2, S])
    yT = nc.dram_tensor("yT", [128, NT, KC * T], F32, kind="ExternalOutput").ap()
    Xs = nc.dram_tensor("Xs", [NT, 128, KC * T], F32).ap()
    Qs = nc.dram_tensor("Qs", [NT, 128, 16 * T], BF16).ap()
    Ks = nc.dram_tensor("Ks", [128, 4, S], BF16).ap()
    Vs = nc.dram_tensor("Vs", [128, 16, 512], BF16).ap()

    with ExitStack() as st:
        def sb(name, shape, dt):
            return st.enter_context(nc.sbuf_tensor(name, shape, dt))

        Xb = sb("Xb", [128, KC * T], F32)
        Hb = sb("Hb", [128, 2 * KC * T], BF16)
        R1 = sb("R1", [128, 32 * T], BF16)
        P2 = sb("P2", [128, 32 * T], BF16)
        SL = sb("SL", [128, NSB * SLAB], BF16)
        NTMP = 8
        TMP = sb("TMP", [128, NTMP * T], F32)
        NTB = 9
        TB = sb("TB", [128, NTB * T], BF16)
        ROPE = sb("ROPE", [128, 4 * T], F32)
        BIAS2 = sb("BIAS2", [128, 2048], F32)
        WST = sb("WST", [128, 2048], BF16)
        ONES = sb("ONES", [128, 128], BF16)
        PM = sb("PM", [128, 128], BF16)
        EPSC = sb("EPSC", [128, 1], F32)
        GAINS = sb("GAINS", [128, 5 * KC], F32)
        LNCOL = sb("LNCOL", [128, 2 * KC], F32)
        QKG = sb("QKG", [128, 4], F32)
        STATS = sb("STATS", [128, 24], F32)
        MV = sb("MV", [128, 2], F32)
        SM = sb("SM", [128, 2], F32)
        PS = st.enter_context(nc.psum_tensor("PS", [128, 8 * 512], F32))

        P = Prog(nc, st)
        pe, act, dve, pool, sp = P.pe, P.act, P.dve, P.pool, P.sp
        op, dma = P.op, P.dma

        def cols(t, u, n=1, w=T):
            return t[:, u * w:(u + n) * w]

        Xu = [Unit(f"X{i}") for i in range(KC)]
        Hu = [[Unit(f"H{j}_{i}") for i in range(KC)] for j in range(2)]
        R1u = [Unit(f"R1_{i}") for i in range(32)]
        P2u = [Unit(f"P2_{i}") for i in range(32)]
        SLu = [Unit(f"SL{i}") for i in range(NSB)]
        TMPu = [Unit(f"TMP{i}") for i in range(NTMP)]
        TBu = [Unit(f"TB{i}") for i in range(NTB)]
        ROPEu = [Unit(f"ROPE{i}") for i in range(4)]
        PSu = [Unit(f"PS{i}") for i in range(8)]
        u_bias2, u_wst, u_ones, u_pm, u_negh = (Unit(n) for n in ("bias2", "wst", "ones", "pm", "negh"))
        u_gains, u_lncol, u_qkg, u_stats, u_mv, u_sm = (Unit(n) for n in ("gains", "lncol", "qkg", "stats", "mv", "sm"))
        Xs_u = [Unit(f"Xs{i}") for i in range(NT)]
        Qs_u = [Unit(f"Qs{i}") for i in range(NT)]
        Ks_u = [Unit(f"Ks{i}") for i in range(NT)]
        Vs_u = [Unit(f"Vs{i}") for i in range(NT)]

        def bank(b, lo=0, n=512):
            return PS[:, b * 512 + lo:b * 512 + lo + n]

        mm = Ring([0, 1, 2, 3])
        aux = Ring([4, 5, 6, 7])
        sring = Ring([0, 1, 2])
        oring = Ring([3, 4])
        dring = Ring([5, 6])
        tmp = Ring([3, 4, 5])
        q32r = Ring([0, 1, 2])
        rsv = Ring([NTMP - 2, NTMP - 1])
        tb = Ring(range(NTB))
        hlr = Ring([(3, 4), (5, 6), (7, 8)])
        sqr = Ring([0, 1, 2])

        slab_order = []
        for tt in range(NT):
            slab_order += list(range(NA))
        for tt in range(NT):
            slab_order += list(range(NA, NA + NB_))
        spec_all = specA + specB
        sl_ch = [P.chan(f"sl{i}") for i in range(NSB)]
        sl_state = {"pos": 0, "loaded": 0}

        def get_slab(kind, hold=0):
            i = sl_state["pos"]
            sl_state["pos"] += 1
            assert spec_all[slab_order[i]][0] == kind, (i, spec_all[slab_order[i]], kind)
            while sl_state["loaded"] < min(len(slab_order), i + NSB - hold):
                j = sl_state["loaded"]
                bi = j % NSB
                dma(pool, sl_ch[bi], cols(SL, bi, 1, SLAB), wsl[slab_order[j]], reads=(), writes=[SLu[bi]])
                sl_state["loaded"] += 1
            bi = i % NSB
            return cols(SL, bi, 1, SLAB), SLu[bi]

        ch_c = [P.chan(f"c{i}") for i in range(6)]
        dma(sp, ch_c[0], GAINS[:], gains_d, writes=[u_gains])
        dma(sp, ch_c[1], LNCOL[:], lncol_d, writes=[u_lncol])
        dma(sp, ch_c[2], QKG[:], qkg_d, writes=[u_qkg])
        dma(pool, ch_c[3], PM[:], pm_d, writes=[u_pm])
        dma(sp, ch_c[4], BIAS2[:], bsb_d, writes=[u_bias2])
        dma(pool, ch_c[5], WST[:], wsT_d, writes=[u_wst])
        op(dve, lambda: nc.vector.memset(ONES[:], 1.0), writes=[u_ones])
        op(dve, lambda: nc.vector.memset(EPSC[:], EPS), writes=[u_negh])
        for gq in range(4):
            b = aux.next()
            op(pe, lambda: nc.tensor.matmul(bank(b), lhsT=ONES[:], rhs=WST[:, gq * 512:(gq + 1) * 512], start=True, stop=True),
               reads=[u_ones, u_wst], writes=[PSu[b]])
            for j in range(4):
                g = gq * 4 + j
                op(dve, lambda: nc.vector.scalar_tensor_tensor(
                    out=BIAS2[:, g * 128:(g + 1) * 128], in0=bank(b, j * 128, 128), scalar=LNCOL[:, KC + g:KC + g + 1],
                    in1=BIAS2[:, g * 128:(g + 1) * 128], op0=ALU.mult, op1=ALU.add),
                   reads=[PSu[b], u_lncol, u_bias2], writes=[u_bias2])

        bg = []

        def inject(n=1):
            for _ in range(n):
                if bg:
                    bg.pop(0)()

        def flush_bg():
            while bg:
                bg.pop(0)()

        hsel = Ring([0, 1])

        def hcols(hb, kc, n=1):
            return Hb[:, (hb * KC + kc) * T:(hb * KC + kc + n) * T]

        def rstd_from(ss, scale, r=None):
            t = tmp.next()
            op(act, lambda: nc.scalar.activation(out=cols(TMP, t), in_=bank(ss), func=AF.Ln, bias=EPSC[:, 0:1], scale=scale),
               reads=[PSu[ss], u_negh], writes=[TMPu[t]])
            if r is None:
                r = tmp.next()
            op(act, lambda: nc.scalar.activation(out=cols(TMP, r), in_=cols(TMP, t), func=AF.Exp, scale=-0.5),
               reads=[TMPu[t]], writes=[TMPu[r]])
            return r

        def rmsnorm_steps(gidx, to_x=False):
            hb = hsel.next()
            hu = Hu[hb]
            state = {}
            steps = []

            def sq_step(q4):
                def f():
                    op(act, lambda: nc.scalar.activation(out=hcols(hb, q4 * 4, 4), in_=cols(Xb, q4 * 4, 4), func=AF.Square),
                       reads=Xu[q4 * 4:q4 * 4 + 4], writes=hu[q4 * 4:q4 * 4 + 4])
                return f

            def sum_step():
                ss = aux.next()
                state["ss"] = ss
                for kc in range(KC):
                    op(pe, lambda: nc.tensor.matmul(bank(ss), lhsT=ONES[:], rhs=hcols(hb, kc), start=(kc == 0), stop=(kc == KC - 1)),
                       reads=[hu[kc], u_ones], writes=[PSu[ss]], signal=(kc == KC - 1))

            def rstd_step():
                state["r"] = rstd_from(state["ss"], 1.0 / D, rsv.next())

            def norm_step(q4):
                def f():
                    r = state["r"]
                    for kc in range(q4 * 4, q4 * 4 + 4):
                        dst, du = (cols(Xb, kc), Xu[kc]) if to_x else (hcols(hb, kc), hu[kc])
                        op(dve, lambda: nc.vector.scalar_tensor_tensor(
                            out=dst, in0=cols(Xb, kc), scalar=GAINS[:, gidx * KC + kc:gidx * KC + kc + 1], in1=cols(TMP, r),
                            op0=ALU.mult, op1=ALU.mult),
                           reads=[Xu[kc], TMPu[r], u_gains], writes=[du])
                return f

            def sum_rstd_step():
                sum_step()
                rstd_step()

            steps += [sq_step(q) for q in range(4)]
            steps += [sum_rstd_step]
            steps += [norm_step(q) for q in range(4)]
            return hb, steps

        def rmsnorm(gidx, to_x=False):
            hb, steps = rmsnorm_steps(gidx, to_x)
            for f in steps:
                f()
            return hb

        def make_acc(gidx, to_x=False):
            hb = hsel.next()
            hu = Hu[hb]
            st_ = {"pend": None, "ss": aux.next(), "n": 0}

            def mm_(n, last):
                ss = st_["ss"]
                op(pe, lambda: nc.tensor.matmul(bank(ss), lhsT=ONES[:], rhs=hcols(hb, n), start=(st_["n"] == 0), stop=last),
                   reads=[hu[n], u_ones], writes=[PSu[ss]], signal=last)
                st_["n"] += 1

            def on_chunk(n):
                op(act, lambda: nc.scalar.activation(out=hcols(hb, n), in_=cols(Xb, n), func=AF.Square),
                   reads=[Xu[n]], writes=[hu[n]])
                if st_["pend"] is not None:
                    mm_(st_["pend"], False)
                st_["pend"] = n

            def tail_steps():
                state = {}

                def first():
                    mm_(st_["pend"], True)
                    assert st_["n"] == KC
                    state["r"] = rstd_from(st_["ss"], 1.0 / D, rsv.next())

                def norm_step(q4):
                    def f():
                        r = state["r"]
                        for kc in range(q4 * 4, q4 * 4 + 4):
                            dst, du = (cols(Xb, kc), Xu[kc]) if to_x else (hcols(hb, kc), hu[kc])
                            op(dve, lambda: nc.vector.scalar_tensor_tensor(
                                out=dst, in0=cols(Xb, kc), scalar=GAINS[:, gidx * KC + kc:gidx * KC + kc + 1], in1=cols(TMP, r),
                                op0=ALU.mult, op1=ALU.mult),
                               reads=[Xu[kc], TMPu[r], u_gains], writes=[du])
                    return f
                return [first] + [norm_step(q) for q in range(4)]

            def finish():
                for f in tail_steps():
                    f()
                return hb

            def evac(n, b):
                resid_evac(n, b)
                on_chunk(n)
            return {"evac": evac, "finish": finish, "tail_steps": tail_steps, "hb": hb}

        def fm_linear(nslabs, rhs_fn, evac, n0=0):
            for j in range(nslabs):
                sl, slu = get_slab("fm")
                for c in range(2):
                    n = n0 + 2 * j + c
                    b = mm.next()
                    for kc in range(KC):
                        rhs, ru = rhs_fn(kc)
                        last = kc == KC - 1
                        op(pe, lambda: nc.tensor.matmul(bank(b), lhsT=sl[:, kc * 256 + c * 128:kc * 256 + c * 128 + 128], rhs=rhs,
                                                        start=(kc == 0), stop=last),
                           reads=[slu, ru], writes=[PSu[b]], signal=last)
                    evac(n, b)
                    inject()

        def tm_linear(n_nb, hb, evac):
            for nb in range(n_nb):
                banks = [mm.next() for _ in range(4)]
                for kh in range(2):
                    sl, slu = get_slab("tm")
                    for tc in range(4):
                        for k8 in range(8):
                            kc = kh * 8 + k8
                            last = kc == KC - 1
                            sig = last or (tc == 3 and k8 == 7)
                            op(pe, lambda: nc.tensor.matmul(bank(banks[tc]), lhsT=hcols(hb, kc)[:, tc * 128:(tc + 1) * 128],
                                                            rhs=sl[:, k8 * 512:(k8 + 1) * 512], start=(kc == 0), stop=last),
                               reads=[slu, Hu[hb][kc]], writes=[PSu[banks[tc]]], signal=sig)
                        inject()
                for tc in range(4):
                    evac(nb, tc, banks[tc])

        def h_rhs(hb):
            return lambda kc: (hcols(hb, kc), Hu[hb][kc])

        def resid_evac(n, b):
            op(dve, lambda: nc.vector.tensor_tensor(out=cols(Xb, n), in0=bank(b), in1=cols(Xb, n), op=ALU.add),
               reads=[PSu[b], Xu[n]], writes=[Xu[n]])

        def ffn(hb, next_gidx, next_to_x=False):
            acc = None
            for half in range(2):
                def ev1(n, b):
                    r = tmp.next()
                    op(act, lambda: nc.scalar.activation(out=cols(TMP, r), in_=bank(b), func=AF.Relu),
                       reads=[PSu[b]], writes=[TMPu[r]])
                    op(dve, lambda: nc.vector.tensor_tensor(out=cols(R1, n), in0=cols(TMP, r), in1=cols(TMP, r), op=ALU.mult),
                       reads=[TMPu[r]], writes=[R1u[n]])
                fm_linear(16, h_rhs(hb), ev1)
                for n in range(16):
                    sl, slu = get_slab("f2")
                    b = mm.next()
                    for kc in range(32):
                        op(pe, lambda: nc.tensor.matmul(bank(b), lhsT=sl[:, kc * 128:(kc + 1) * 128], rhs=cols(R1, kc),
                                                        start=(kc == 0), stop=(kc == 31)),
                           reads=[slu, R1u[kc]], writes=[PSu[b]], signal=(kc == 31))
                    if half == 1:
                        if acc is None:
                            acc = make_acc(next_gidx, next_to_x)
                        acc["evac"](n, b)
                    else:
                        resid_evac(n, b)
                    inject()
            return acc

        def dump_x(tt):
            ch = P.chan("dbg")
            dma(sp, ch, yT[:, tt], Xb[:], reads=Xu, writes=[])
            return ch

        out_ch = P.chan("out")
        chX = P.chan("X")
        chR = [P.chan(f"rope{i}") for i in range(4)]
        chQ, chK, chV = P.chan("Q"), P.chan("K"), P.chan("V")
        dbg_ch = []

        def load_x_A(tt):
            dma(sp, chX, Xb[:], xT[:, tt], writes=Xu)

        ntA = NT if dbg is None else 1
        hb_next = None
        for tt in range(ntA):
            if tt == 0:
                load_x_A(0)
                hb = rmsnorm(0)
            else:
                hb = hb_next

            def v_evac(nb, tc, b):
                v32 = R1[:, tc * 4096:(tc + 1) * 4096].bitcast(F32)
                op(act, lambda: nc.scalar.activation(out=v32[:, nb * 512:(nb + 1) * 512], in_=bank(b), func=AF.Gelu),
                   reads=[PSu[b]], writes=[R1u[tc * 8 + nb * 2], R1u[tc * 8 + nb * 2 + 1]])
            tm_linear(4, hb, v_evac)

            def ln_step(tc):
                def f():
                    v32 = R1[:, tc * 4096:(tc + 1) * 4096].bitcast(F32)
                    for nb in range(4):
                        op(dve, lambda: nc.vector.bn_stats(out=STATS[:, nb * 6:(nb + 1) * 6], in_=v32[:, nb * 512:(nb + 1) * 512]),
                           reads=[R1u[tc * 8 + nb * 2], R1u[tc * 8 + nb * 2 + 1]], writes=[u_stats])
                    op(dve, lambda: nc.vector.bn_aggr(out=MV[:], in_=STATS[:]), reads=[u_stats], writes=[u_mv])
                    op(act, lambda: nc.scalar.activation(out=SM[:, 0:1], in_=MV[:, 1:2], func=AF.Ln, bias=EPSC[:, 0:1], scale=1.0),
                       reads=[u_mv, u_negh], writes=[u_sm])
                    op(act, lambda: nc.scalar.activation(out=SM[:, 1:2], in_=SM[:, 0:1], func=AF.Exp, scale=-0.5),
                       reads=[u_sm], writes=[u_sm])
                    op(dve, lambda: nc.vector.tensor_scalar(out=cols(P2, 16 + 4 * tc, 4), in0=v32, scalar1=MV[:, 0:1], scalar2=SM[:, 1:2],
                                                            op0=ALU.subtract, op1=ALU.mult),
                       reads=R1u[tc * 8:tc * 8 + 8] + [u_mv, u_sm], writes=P2u[16 + 4 * tc:16 + 4 * tc + 4])
                return f
            for tc in range(4):
                bg.append(ln_step(tc))
                bg.append(lambda: None)
            fm_linear(8, h_rhs(hb),
                      lambda n, b: op(act, lambda: nc.scalar.activation(out=cols(P2, n), in_=bank(b), func=AF.Gelu),
                                      reads=[PSu[b]], writes=[P2u[n]]))
            flush_bg()
            sl01 = [get_slab("fm"), get_slab("fm", hold=1)]
            wbanks = [mm.next() for _ in range(4)]
            sbanks = {}

            def spatial(gq):
                for tc in range(4):
                    vn = cols(P2, 16 + 4 * tc, 4)
                    b = aux.next()
                    sbanks[(gq, tc)] = b
                    for j in range(4):
                        g = gq * 4 + j
                        op(pe, lambda: nc.tensor.matmul(bank(b, j * 128, 128), lhsT=vn[:, g * 128:(g + 1) * 128],
                                                        rhs=WST[:, g * 128:(g + 1) * 128], start=True, stop=True),
                           reads=[P2u[16 + 4 * tc + gq], u_wst], writes=[PSu[b]], signal=(j == 3))

            def gating(gq):
                for tc in range(4):
                    b = sbanks[(gq, tc)]
                    t = tmp.next()
                    for j in range(4):
                        g = gq * 4 + j
                        op(dve, lambda: nc.vector.scalar_tensor_tensor(
                            out=TMP[:, t * T + j * 128:t * T + (j + 1) * 128], in0=bank(b, j * 128, 128), scalar=LNCOL[:, g:g + 1],
                            in1=BIAS2[:, g * 128:(g + 1) * 128], op0=ALU.mult, op1=ALU.add),
                           reads=[PSu[b], u_lncol, u_bias2], writes=[TMPu[t]])
                    uview = cols(P2, gq * 4, 4).rearrange("p (g t) -> p g t", t=T)[:, :, tc * 128:(tc + 1) * 128]
                    tview = cols(TMP, t).rearrange("p (g t) -> p g t", t=128)
                    op(dve, lambda: nc.vector.tensor_tensor(out=uview, in0=tview, in1=uview, op=ALU.mult),
                       reads=[TMPu[t]] + P2u[gq * 4:gq * 4 + 4], writes=P2u[gq * 4:gq * 4 + 4])

            def wout_stage(gq):
                for n in range(4):
                    sl, slu = sl01[n // 2]
                    for kc in range(4 * gq, 4 * gq + 4):
                        last = kc == KC - 1
                        c = n % 2
                        op(pe, lambda: nc.tensor.matmul(bank(wbanks[n]), lhsT=sl[:, kc * 256 + c * 128:kc * 256 + c * 128 + 128],
                                                        rhs=cols(P2, kc), start=(kc == 0), stop=last),
                           reads=[slu, P2u[kc]], writes=[PSu[wbanks[n]]], signal=last)

            spatial(0)
            for gq in range(4):
                gating(gq)
                if gq + 1 < 4:
                    spatial(gq + 1)
                wout_stage(gq)
            acc1 = make_acc(1)
            for n in range(4):
                acc1["evac"](n, wbanks[n])
            fm_linear(6, lambda kc: (cols(P2, kc), P2u[kc]), acc1["evac"], n0=4)
            if dbg == "mix0":
                dbg_ch.append(dump_x(tt))
                break
            hb1 = acc1["finish"]()

            def rope_load(tt=tt):
                for i in range(4):
                    dma(sp, chR[i], cols(ROPE, i), rope_d[:, i % 2, tt * T:(tt + 1) * T], writes=[ROPEu[i]])

            def rope_scale():
                for i in range(4):
                    op(dve, lambda: nc.vector.tensor_scalar(out=cols(ROPE, i), in0=cols(ROPE, i), scalar1=QKG[:, i:i + 1], scalar2=None,
                                                            op0=ALU.mult),
                       reads=[ROPEu[i], u_qkg], writes=[ROPEu[i]])
            bg.append(rope_load)
            bg.extend([lambda: None] * 8)
            bg.append(rope_scale)
            acc2 = ffn(hb1, 2)
            flush_bg()
            dma(sp, chX, Xs[tt], Xb[:], reads=Xu, writes=[Xs_u[tt]])
            if dbg == "ffn0":
                dbg_ch.append(dump_x(tt))
                break

            hb2 = acc2["finish"]()
            if tt + 1 < ntA:
                hb_next, nsteps = rmsnorm_steps(0)
                bg.append(lambda: load_x_A(tt + 1))
                bg.extend([lambda: None] * 5)
                bg.extend(nsteps)
            pend = []

            def qk_post(kind):
                ci = 0 if kind == "q" else 2

                def ev(n, b):
                    outu = n if kind == "q" else 16 + n
                    q32 = q32r.next()
                    op(act, lambda: nc.scalar.activation(out=cols(TMP, q32), in_=bank(b), func=AF.Copy),
                       reads=[PSu[b]], writes=[TMPu[q32]])
                    sq = sqr.next()
                    op(act, lambda: nc.scalar.activation(out=cols(TB, sq), in_=bank(b), func=AF.Square),
                       reads=[PSu[b]], writes=[TBu[sq]])
                    hi, lo = hlr.next()
                    op(act, lambda: nc.scalar.activation(out=cols(TB, hi), in_=bank(b), func=AF.Copy),
                       reads=[PSu[b]], writes=[TBu[hi]])
                    op(pool, lambda: nc.gpsimd.tensor_tensor(out=cols(TB, lo), in0=cols(TMP, q32), in1=cols(TB, hi), op=ALU.subtract),
                       reads=[TMPu[q32], TBu[hi]], writes=[TBu[lo]])

                    def stage2():
                        ss = aux.next()
                        op(pe, lambda: nc.tensor.matmul(bank(ss), lhsT=ONES[:], rhs=cols(TB, sq), start=True, stop=True),
                           reads=[TBu[sq], u_ones], writes=[PSu[ss]])
                        pp = aux.next()
                        op(pe, lambda: nc.tensor.matmul(bank(pp), lhsT=PM[:], rhs=cols(TB, hi), start=True, stop=False),
                           reads=[TBu[hi], u_pm], writes=[PSu[pp]], signal=False)
                        op(pe, lambda: nc.tensor.matmul(bank(pp), lhsT=PM[:], rhs=cols(TB, lo), start=False, stop=True),
                           reads=[TBu[lo], u_pm], writes=[PSu[pp]])
                        r = rstd_from(ss, 1.0 / 128)
                        t = tmp.next()
                        a = tmp.next()
                        op(dve, lambda: nc.vector.tensor_tensor(out=cols(TMP, a), in0=cols(TMP, q32), in1=cols(ROPE, ci), op=ALU.mult),
                           reads=[TMPu[q32], ROPEu[ci]], writes=[TMPu[a]])
                        op(dve, lambda: nc.vector.tensor_tensor(out=cols(TMP, t), in0=bank(pp), in1=cols(ROPE, ci + 1), op=ALU.mult),
                           reads=[PSu[pp], ROPEu[ci + 1], TMPu[t]], writes=[TMPu[t]])
                        op(dve, lambda: nc.vector.tensor_tensor(out=cols(TMP, a), in0=cols(TMP, a), in1=cols(TMP, t), op=ALU.add),
                           reads=[TMPu[a], TMPu[t]], writes=[TMPu[a]])
                        op(dve, lambda: nc.vector.tensor_tensor(out=cols(R1, outu), in0=cols(TMP, a), in1=cols(TMP, r), op=ALU.mult),
                           reads=[TMPu[a], TMPu[r]], writes=[R1u[outu]])
                    if pend:
                        pend.pop()()
                    pend.append(stage2)
                return ev

            fm_linear(8, h_rhs(hb2), qk_post("q"))
            fm_linear(2, h_rhs(hb2), qk_post("k"))
            if pend:
                pend.pop()()
            tm_linear(1, hb2, lambda nb, tc, b: op(act, lambda: nc.scalar.activation(out=cols(R1, 20 + tc), in_=bank(b), func=AF.Copy),
                                                   reads=[PSu[b]], writes=[R1u[20 + tc]]))
            flush_bg()
            dma(sp, chQ, Qs[tt], cols(R1, 0, 16), reads=R1u[0:16], writes=[Qs_u[tt]])
            dma(sp, chK, Ks[:, :, tt * T:(tt + 1) * T], cols(R1, 16, 4).rearrange("p (h t) -> p h t", t=T),
                reads=R1u[16:20], writes=[Ks_u[tt]])
            dma(sp, chV, Vs[:, tt * 4:(tt + 1) * 4, :], cols(R1, 20, 4).rearrange("p (c f) -> p c f", f=512),
                reads=R1u[20:24], writes=[Vs_u[tt]])

        ntB = NT if dbg is None else 0
        if ntB:
            chKV = [P.chan("KVa"), P.chan("KVb")]
            dma(sp, chKV[0], cols(P2, 0, 16), Ks.rearrange("p h t -> p (h t)"), reads=Ks_u, writes=P2u[0:16])
            dma(sp, chKV[1], cols(P2, 16, 16), Vs.rearrange("p c f -> p (c f)"), reads=Vs_u, writes=P2u[16:32])
            dma(sp, chQ, cols(R1, 0, 16), Qs[0], reads=[Qs_u[0]], writes=R1u[0:16])
            dma(sp, chX, Xb[:], Xs[0], reads=[Xs_u[0]], writes=Xu)
        for tt in range(ntB):
            NG = 16 * 8

            def s_pair(g):
                h, j = divmod(g, 8)
                kvh = h // 4
                sb0 = 2 * (g % 2)
                for i in range(2):
                    kc = 2 * j + i
                    op(pe, lambda: nc.tensor.matmul(bank(sb0 + i), lhsT=P2[:, kvh * S + kc * 128:kvh * S + (kc + 1) * 128], rhs=cols(R1, h),
                                                    start=True, stop=True),
                       reads=[P2u[kvh * 4 + kc // 4], R1u[h]], writes=[PSu[sb0 + i]], signal=(i == 1))

            def exp_pair(g):
                sb0 = 2 * (g % 2)
                p0 = 2 * (g % 3)
                op(act, lambda: nc.scalar.activation(out=cols(TB, p0, 2), in_=PS[:, sb0 * 512:(sb0 + 2) * 512], func=AF.Exp, scale=SCALE),
                   reads=[PSu[sb0], PSu[sb0 + 1]], writes=[TBu[p0], TBu[p0 + 1]])

            def od_pair(g):
                h, j = divmod(g, 8)
                kvh = h // 4
                p0 = 2 * (g % 3)
                ob, db = 4 + h % 2, 6 + h % 2
                for i in range(2):
                    kc = 2 * j + i
                    op(pe, lambda: nc.tensor.matmul(bank(ob), lhsT=P2[:, (16 + kc) * T + kvh * 128:(16 + kc) * T + (kvh + 1) * 128],
                                                    rhs=cols(TB, p0 + i), start=(kc == 0), stop=(kc == 15)),
                       reads=[P2u[16 + kc], TBu[p0 + i]], writes=[PSu[ob]], signal=(kc == 15))
                    op(pe, lambda: nc.tensor.matmul(bank(db), lhsT=ONES[:], rhs=cols(TB, p0 + i), start=(kc == 0), stop=(kc == 15)),
                       reads=[TBu[p0 + i], u_ones], writes=[PSu[db]], signal=(i == 1))
                if j == 7:
                    rd = tmp.next()
                    op(dve, lambda: nc.vector.reciprocal(out=cols(TMP, rd), in_=bank(db)), reads=[PSu[db]], writes=[TMPu[rd]])
                    op(dve, lambda: nc.vector.tensor_tensor(out=cols(R1, 16 + h), in0=bank(ob), in1=cols(TMP, rd), op=ALU.mult),
                       reads=[PSu[ob], TMPu[rd]], writes=[R1u[16 + h]])
                    inject()

            s_pair(0)
            for g in range(NG + 1):
                if g < NG:
                    exp_pair(g)
                if g + 1 < NG:
                    s_pair(g + 1)
                if g >= 1:
                    od_pair(g - 1)
            flush_bg()
            acc3 = make_acc(3)
            fm_linear(8, lambda kc: (cols(R1, 16 + kc), R1u[16 + kc]), acc3["evac"])
            hb3 = acc3["finish"]()
            acc4 = ffn(hb3, 4, True)
            last = tt + 1 == ntB
            if not last:
                dma(sp, chQ, cols(R1, 0, 16), Qs[tt + 1], reads=[Qs_u[tt + 1]], writes=R1u[0:16])
            fsteps = acc4["tail_steps"]()
            fsteps.pop(0)()

            def store_step(tt=tt):
                dma(sp, out_ch, yT[:, tt], Xb[:], reads=Xu, writes=[])

            def xload_step(tt=tt):
                dma(sp, chX, Xb[:], Xs[tt + 1], reads=[Xs_u[tt + 1]], writes=Xu)
            fsteps.append(store_step)
            if not last:
                fsteps.append(xload_step)
                bg.extend(fsteps)
            else:
                for f in fsteps:
                    f()

        for ch in [out_ch] + dbg_ch:
            if ch.count:
                sp.e.wait_ge(ch.sem, ch.count)
        for e in (pe, act, dve, pool):
            if e.count:
                sp.e.wait_ge(e.sem, e.count)
    return nc


def rope_tables():
    half = 32
    inv_freq = (10000.0 ** (-np.arange(0, 64, 2, dtype=np.float32) / np.float32(64))).astype(np.float32)
    t = np.arange(S)
    row = (t // 64).astype(np.float32)
    col = (t % 64).astype(np.float32)
    cos = np.zeros((128, S), np.float32)
    sin = np.zeros((128, S), np.float32)
    for d in range(128):
        pos = row if d < 64 else col
        j = d % 64
        ang = (pos * inv_freq[j % half]).astype(np.float32)
        cos[d] = np.cos(ang)
        sgn = -1.0 if j < half else 1.0
        sin[d] = sgn * np.sin(ang)
    return np.stack([cos, sin], axis=1).astype(np.float32)


def partner_perm():
    p = np.arange(128)
    j = p % 64
    return np.where(j < 32, p + 32, p - 32)


def prep_shared(inputs):
    W = {k: np.asarray(v, dtype=np.float32) for k, v in inputs.items() if k != "x"}
    specA, specB = slab_specs()
    wsl = np.empty((len(specA) + len(specB), 128, SLAB), np.float32)
    for i, sp_ in enumerate(specA + specB):
        wsl[i] = pack_slab(sp_, W)

    def colmajor(v):
        return v.reshape(KC, 128).T

    gains = np.concatenate([colmajor(W["norm_mix"][0]), colmajor(W["norm_ffn"][0]), colmajor(W["norm_mix"][1]),
                            colmajor(W["norm_ffn"][1]), colmajor(W["norm_final"])], axis=1)
    lncol = np.concatenate([colmajor(W["gm_ln_g"][0]), colmajor(W["gm_ln_b"][0])], axis=1)
    wsT = W["gm_w_s"][0].transpose(2, 0, 1).reshape(128, 16 * 128)
    bsb = np.broadcast_to(W["gm_b_s"][0].reshape(1, 16 * 128), (128, 16 * 128))
    perm = partner_perm()
    gq, gk = W["attn_q_norm"][0], W["attn_k_norm"][0]
    qkg = np.stack([gq, gq[perm], gk, gk[perm]], axis=1)
    pm = np.zeros((128, 128), np.float32)
    pm[perm, np.arange(128)] = 1.0
    c = np.ascontiguousarray
    return {"wsl": wsl, "gains": c(gains, np.float32), "lncol": c(lncol, np.float32), "wsT": c(wsT, np.float32),
            "bsb": c(bsb, np.float32), "qkg": c(qkg, np.float32), "pm": pm, "rope": rope_tables()}


def to_fm(xb):
    return np.ascontiguousarray(xb.reshape(NT, T, KC, 128).transpose(3, 0, 2, 1).reshape(128, NT, KC * T))


def from_fm(y):
    return np.ascontiguousarray(y.reshape(128, NT, KC, T).transpose(1, 3, 2, 0).reshape(S, D))


def kernel(**inputs):
    x = np.asarray(inputs["x"], dtype=np.float32)
    shared = prep_shared(inputs)
    nc = build_program(DBG)
    in_maps = []
    for b in range(8):
        m = dict(shared)
        m["xT"] = to_fm(x[b])
        in_maps.append(m)
    res = run_bass_kernel_spmd(nc, in_maps, core_ids=list(range(8)))
    out = np.stack([from_fm(np.asarray(res.results[b]["yT"], dtype=np.float32)) for b in range(8)], axis=0)
    return out.astype(np.float32)
```

```python
import math
from contextlib import ExitStack

import numpy as np
import concourse.bass as bass
import concourse.mybir as mybir
from concourse.bass_utils import run_bass_kernel_spmd

F32 = mybir.dt.float32
BF16 = mybir.dt.bfloat16
AF = mybir.ActivationFunctionType
ALU = mybir.AluOpType

D = 2048
S = 2048
T = 512
NT = S // T
KC = D // 128
DFF = 8192
EPS = 1e-6
NSB = 4
SLAB = 4096
SCALE = 128.0 ** -0.5

DBG = None


def slab_specs():
    A = []
    for nb in range(4):
        for kh in range(2):
            A.append(("tm", "gm_w_in", 0, 2048 + nb * 512, kh))
    for j in range(8):
        A.append(("fm", "gm_w_in", 0, j * 256))
    for j in range(8):
        A.append(("fm", "gm_w_out", 0, j * 256))
    for half in range(2):
        for j in range(16):
            A.append(("fm", "ffn_w1", 0, half * 4096 + j * 256))
        for n in range(16):
            A.append(("f2", "ffn_w2", 0, half, n))
    for j in range(8):
        A.append(("fm", "attn_w_qkv", 0, j * 256))
    for j in range(2):
        A.append(("fm", "attn_w_qkv", 0, 2048 + j * 256))
    for kh in range(2):
        A.append(("tm", "attn_w_qkv", 0, 2560, kh))
    B = []
    for j in range(8):
        B.append(("fm", "attn_w_o", 0, j * 256))
    for half in range(2):
        for j in range(16):
            B.append(("fm", "ffn_w1", 1, half * 4096 + j * 256))
        for n in range(16):
            B.append(("f2", "ffn_w2", 1, half, n))
    return A, B


def pack_slab(spec, W):
    kind = spec[0]
    w = W[spec[1]][spec[2]]
    if kind == "fm":
        c0 = spec[3]
        blk = w[:, c0:c0 + 256]
        return blk.reshape(16, 128, 256).transpose(1, 0, 2).reshape(128, SLAB)
    if kind == "tm":
        c0, kh = spec[3], spec[4]
        blk = w[kh * 1024:(kh + 1) * 1024, c0:c0 + 512]
        return blk.reshape(8, 128, 512).transpose(1, 0, 2).reshape(128, SLAB)
    half, n = spec[3], spec[4]
    blk = w[half * 4096:(half + 1) * 4096, n * 128:(n + 1) * 128]
    return blk.reshape(32, 128, 128).transpose(1, 0, 2).reshape(128, SLAB)


class Tok:
    __slots__ = ("sem", "val", "own")

    def __init__(self, sem, val, own):
        self.sem, self.val, self.own = sem, val, own


class Unit:
    __slots__ = ("w", "r", "name")

    def __init__(self, name):
        self.w = None
        self.r = {}
        self.name = name


class Eng:
    def __init__(self, name, e, sem, is_pe=False):
        self.name, self.e, self.sem, self.is_pe = name, e, sem, is_pe
        self.count = 0
        self.pending = []
        self.seen = {}


class Chan:
    def __init__(self, sem):
        self.sem = sem
        self.count = 0


class Prog:
    def __init__(self, nc, stack):
        self.nc = nc
        self.stack = stack
        self.nsem = 0
        self.pe = Eng("pe", nc.tensor, self.sem("pe"), True)
        self.act = Eng("act", nc.scalar, self.sem("act"))
        self.dve = Eng("dve", nc.vector, self.sem("dve"))
        self.pool = Eng("pool", nc.gpsimd, self.sem("pool"))
        self.sp = Eng("sp", nc.sync, self.sem("sp"))

    def sem(self, name):
        self.nsem += 1
        return self.stack.enter_context(self.nc.semaphore(f"s{self.nsem}_{name}"))

    def chan(self, name):
        return Chan(self.sem("ch_" + name))

    def _waits(self, eng, reads, writes):
        need = {}

        def add(tok, kind):
            if tok is None:
                return
            if tok.own is eng:
                if eng.is_pe or kind != "raw":
                    return
            assert tok.val is not None, f"dependency on unsignalled instruction ({eng.name})"
            k = id(tok.sem)
            if eng.seen.get(k, 0) >= tok.val:
                return
            if k not in need or need[k][1] < tok.val:
                need[k] = (tok.sem, tok.val)

        for u in reads:
            add(u.w, "raw")
        for u in writes:
            add(u.w, "waw")
            for t in u.r.values():
                add(t, "war")
        for sem, val in need.values():
            eng.e.wait_ge(sem, val)
            eng.seen[id(sem)] = val

    def op(self, eng, build, reads=(), writes=(), signal=True):
        self._waits(eng, reads, writes)
        ins = build()
        tok = Tok(eng.sem, None, eng)
        if signal:
            eng.count += 1
            ins.then_inc(eng.sem, 1)
            tok.val = eng.count
            for p in eng.pending:
                p.val = eng.count
            eng.pending = []
        else:
            eng.pending.append(tok)
        for u in reads:
            u.r[id(eng.sem)] = tok
        for u in writes:
            u.w = tok
            u.r = {}
        return tok

    def dma(self, q, ch, out, in_, reads=(), writes=()):
        self._waits(q, reads, writes)
        ins = q.e.dma_start(out=out, in_=in_)
        ch.count += 16
        ins.then_inc(ch.sem, 16)
        tok = Tok(ch.sem, ch.count, None)
        for u in reads:
            u.r[id(ch.sem)] = tok
        for u in writes:
            u.w = tok
            u.r = {}
        return tok


class Ring:
    def __init__(self, items):
        self.items = list(items)
        self.i = 0

    def next(self):
        v = self.items[self.i % len(self.items)]
        self.i += 1
        return v


def build_program(dbg=None):
    nc = bass.Bass("TRN2", target_bir_lowering=False)
    specA, specB = slab_specs()
    NA, NB_ = len(specA), len(specB)

    def din(name, shape, dt=F32):
        return nc.dram_tensor(name, shape, dt, kind="ExternalInput").ap()

    xT = din("xT", [128, NT, KC * T])
    wsl = din("wsl", [NA + NB_, 128, SLAB])
    gains_d = din("gains", [128, 5 * KC])
    lncol_d = din("lncol", [128, 2 * KC])
    wsT_d = din("wsT", [128, 16 * 128])
    bsb_d = din("bsb", [128, 16 * 128])
    qkg_d = din("qkg", [128, 4])
    pm_d = din("pm", [128, 128])
    rope_d = din("rope", [128, 2, S])
    yT = nc.dram_tensor("yT", [128, NT, KC * T], F32, kind="ExternalOutput").ap()
    Xs = nc.dram_tensor("Xs", [NT, 128, KC * T], F32).ap()
    Qs = nc.dram_tensor("Qs", [NT, 128, 16 * T], BF16).ap()
    Ks = nc.dram_tensor("Ks", [128, 4, S], BF16).ap()
    Vs = nc.dram_tensor("Vs", [128, 16, 512], BF16).ap()

    with ExitStack() as st:
        def sb(name, shape, dt):
            return st.enter_context(nc.sbuf_tensor(name, shape, dt))

        Xb = sb("Xb", [128, KC * T], F32)
        Hb = sb("Hb", [128, 2 * KC * T], BF16)
        R1 = sb("R1", [128, 32 * T], BF16)
        P2 = sb("P2", [128, 32 * T], BF16)
        SL = sb("SL", [128, NSB * SLAB], BF16)
        NTMP = 8
        TMP = sb("TMP", [128, NTMP * T], F32)
        NTB = 9
        TB = sb("TB", [128, NTB * T], BF16)
        ROPE = sb("ROPE", [128, 4 * T], F32)
        BIAS2 = sb("BIAS2", [128, 2048], F32)
        WST = sb("WST", [128, 2048], BF16)
        ONES = sb("ONES", [128, 128], BF16)
        PM = sb("PM", [128, 128], BF16)
        EPSC = sb("EPSC", [128, 1], F32)
        GAINS = sb("GAINS", [128, 5 * KC], F32)
        LNCOL = sb("LNCOL", [128, 2 * KC], F32)
        QKG = sb("QKG", [128, 4], F32)
        STATS = sb("STATS", [128, 24], F32)
        MV = sb("MV", [128, 2], F32)
        SM = sb("SM", [128, 2], F32)
        PS = st.enter_context(nc.psum_tensor("PS", [128, 8 * 512], F32))

        P = Prog(nc, st)
        pe, act, dve, pool, sp = P.pe, P.act, P.dve, P.pool, P.sp
        op, dma = P.op, P.dma

        def cols(t, u, n=1, w=T):
            return t[:, u * w:(u + n) * w]

        Xu = [Unit(f"X{i}") for i in range(KC)]
        Hu = [[Unit(f"H{j}_{i}") for i in range(KC)] for j in range(2)]
        R1u = [Unit(f"R1_{i}") for i in range(32)]
        P2u = [Unit(f"P2_{i}") for i in range(32)]
        SLu = [Unit(f"SL{i}") for i in range(NSB)]
        TMPu = [Unit(f"TMP{i}") for i in range(NTMP)]
        TBu = [Unit(f"TB{i}") for i in range(NTB)]
        ROPEu = [Unit(f"ROPE{i}") for i in range(4)]
        PSu = [Unit(f"PS{i}") for i in range(8)]
        u_bias2, u_wst, u_ones, u_pm, u_negh = (Unit(n) for n in ("bias2", "wst", "ones", "pm", "negh"))
        u_gains, u_lncol, u_qkg, u_stats, u_mv, u_sm = (Unit(n) for n in ("gains", "lncol", "qkg", "stats", "mv", "sm"))
        Xs_u = [Unit(f"Xs{i}") for i in range(NT)]
        Qs_u = [Unit(f"Qs{i}") for i in range(NT)]
        Ks_u = [Unit(f"Ks{i}") for i in range(NT)]
        Vs_u = [Unit(f"Vs{i}") for i in range(NT)]

        def bank(b, lo=0, n=512):
            return PS[:, b * 512 + lo:b * 512 + lo + n]

        mm = Ring([0, 1, 2, 3])
        aux = Ring([4, 5, 6, 7])
        sring = Ring([0, 1, 2])
        oring = Ring([3, 4])
        dring = Ring([5, 6])
        tmp = Ring([3, 4, 5])
        q32r = Ring([0, 1, 2])
        rsv = Ring([NTMP - 2, NTMP - 1])
        tb = Ring(range(NTB))
        hlr = Ring([(3, 4), (5, 6), (7, 8)])
        sqr = Ring([0, 1, 2])

        slab_order = []
        for tt in range(NT):
            slab_order += list(range(NA))
        for tt in range(NT):
            slab_order += list(range(NA, NA + NB_))
        spec_all = specA + specB
        sl_ch = [P.chan(f"sl{i}") for i in range(NSB)]
        sl_state = {"pos": 0, "loaded": 0}

        def get_slab(kind, hold=0):
            i = sl_state["pos"]
            sl_state["pos"] += 1
            assert spec_all[slab_order[i]][0] == kind, (i, spec_all[slab_order[i]], kind)
            while sl_state["loaded"] < min(len(slab_order), i + NSB - hold):
                j = sl_state["loaded"]
                bi = j % NSB
                dma(pool, sl_ch[bi], cols(SL, bi, 1, SLAB), wsl[slab_order[j]], reads=(), writes=[SLu[bi]])
                sl_state["loaded"] += 1
            bi = i % NSB
            return cols(SL, bi, 1, SLAB), SLu[bi]

        ch_c = [P.chan(f"c{i}") for i in range(6)]
        dma(sp, ch_c[0], GAINS[:], gains_d, writes=[u_gains])
        dma(sp, ch_c[1], LNCOL[:], lncol_d, writes=[u_lncol])
        dma(sp, ch_c[2], QKG[:], qkg_d, writes=[u_qkg])
        dma(pool, ch_c[3], PM[:], pm_d, writes=[u_pm])
        dma(sp, ch_c[4], BIAS2[:], bsb_d, writes=[u_bias2])
        dma(pool, ch_c[5], WST[:], wsT_d, writes=[u_wst])
        op(dve, lambda: nc.vector.memset(ONES[:], 1.0), writes=[u_ones])
        op(dve, lambda: nc.vector.memset(EPSC[:], EPS), writes=[u_negh])
        for gq in range(4):
            b = aux.next()
            op(pe, lambda: nc.tensor.matmul(bank(b), lhsT=ONES[:], rhs=WST[:, gq * 512:(gq + 1) * 512], start=True, stop=True),
               reads=[u_ones, u_wst], writes=[PSu[b]])
            for j in range(4):
                g = gq * 4 + j
                op(dve, lambda: nc.vector.scalar_tensor_tensor(
                    out=BIAS2[:, g * 128:(g + 1) * 128], in0=bank(b, j * 128, 128), scalar=LNCOL[:, KC + g:KC + g + 1],
                    in1=BIAS2[:, g * 128:(g + 1) * 128], op0=ALU.mult, op1=ALU.add),
                   reads=[PSu[b], u_lncol, u_bias2], writes=[u_bias2])

        bg = []

        def inject(n=1):
            for _ in range(n):
                if bg:
                    bg.pop(0)()

        def flush_bg():
            while bg:
                bg.pop(0)()

        hsel = Ring([0, 1])

        def hcols(hb, kc, n=1):
            return Hb[:, (hb * KC + kc) * T:(hb * KC + kc + n) * T]

        def rstd_from(ss, scale, r=None):
            t = tmp.next()
            op(act, lambda: nc.scalar.activation(out=cols(TMP, t), in_=bank(ss), func=AF.Ln, bias=EPSC[:, 0:1], scale=scale),
               reads=[PSu[ss], u_negh], writes=[TMPu[t]])
            if r is None:
                r = tmp.next()
            op(act, lambda: nc.scalar.activation(out=cols(TMP, r), in_=cols(TMP, t), func=AF.Exp, scale=-0.5),
               reads=[TMPu[t]], writes=[TMPu[r]])
            return r

        def rmsnorm_steps(gidx, to_x=False):
            hb = hsel.next()
            hu = Hu[hb]
            state = {}
            steps = []

            def sq_step(q4):
                def f():
                    op(act, lambda: nc.scalar.activation(out=hcols(hb, q4 * 4, 4), in_=cols(Xb, q4 * 4, 4), func=AF.Square),
                       reads=Xu[q4 * 4:q4 * 4 + 4], writes=hu[q4 * 4:q4 * 4 + 4])
                return f

            def sum_step():
                ss = aux.next()
                state["ss"] = ss
                for kc in range(KC):
                    op(pe, lambda: nc.tensor.matmul(bank(ss), lhsT=ONES[:], rhs=hcols(hb, kc), start=(kc == 0), stop=(kc == KC - 1)),
                       reads=[hu[kc], u_ones], writes=[PSu[ss]], signal=(kc == KC - 1))

            def rstd_step():
                state["r"] = rstd_from(state["ss"], 1.0 / D, rsv.next())

            def norm_step(q4):
                def f():
                    r = state["r"]
                    for kc in range(q4 * 4, q4 * 4 + 4):
                        dst, du = (cols(Xb, kc), Xu[kc]) if to_x else (hcols(hb, kc), hu[kc])
                        op(dve, lambda: nc.vector.scalar_tensor_tensor(
                            out=dst, in0=cols(Xb, kc), scalar=GAINS[:, gidx * KC + kc:gidx * KC + kc + 1], in1=cols(TMP, r),
                            op0=ALU.mult, op1=ALU.mult),
                           reads=[Xu[kc], TMPu[r], u_gains], writes=[du])
                return f

            def sum_rstd_step():
                sum_step()
                rstd_step()

            steps += [sq_step(q) for q in range(4)]
            steps += [sum_rstd_step]
            steps += [norm_step(q) for q in range(4)]
            return hb, steps

        def rmsnorm(gidx, to_x=False):
            hb, steps = rmsnorm_steps(gidx, to_x)
            for f in steps:
                f()
            return hb

        def make_acc(gidx, to_x=False):
            hb = hsel.next()
            hu = Hu[hb]
            st_ = {"pend": None, "ss": aux.next(), "n": 0}

            def mm_(n, last):
                ss = st_["ss"]
                op(pe, lambda: nc.tensor.matmul(bank(ss), lhsT=ONES[:], rhs=hcols(hb, n), start=(st_["n"] == 0), stop=last),
                   reads=[hu[n], u_ones], writes=[PSu[ss]], signal=last)
                st_["n"] += 1

            def on_chunk(n):
                op(act, lambda: nc.scalar.activation(out=hcols(hb, n), in_=cols(Xb, n), func=AF.Square),
                   reads=[Xu[n]], writes=[hu[n]])
                if st_["pend"] is not None:
                    mm_(st_["pend"], False)
                st_["pend"] = n

            def tail_steps():
                state = {}

                def first():
                    mm_(st_["pend"], True)
                    assert st_["n"] == KC
                    state["r"] = rstd_from(st_["ss"], 1.0 / D, rsv.next())

                def norm_step(q4):
                    def f():
                        r = state["r"]
                        for kc in range(q4 * 4, q4 * 4 + 4):
                            dst, du = (cols(Xb, kc), Xu[kc]) if to_x else (hcols(hb, kc), hu[kc])
                            op(dve, lambda: nc.vector.scalar_tensor_tensor(
                                out=dst, in0=cols(Xb, kc), scalar=GAINS[:, gidx * KC + kc:gidx * KC + kc + 1], in1=cols(TMP, r),
                                op0=ALU.mult, op1=ALU.mult),
                               reads=[Xu[kc], TMPu[r], u_gains], writes=[du])
                    return f
                return [first] + [norm_step(q) for q in range(4)]

            def finish():
                for f in tail_steps():
                    f()
                return hb

            def evac(n, b):
                resid_evac(n, b)
                on_chunk(n)
            return {"evac": evac, "finish": finish, "tail_steps": tail_steps, "hb": hb}

        def fm_linear(nslabs, rhs_fn, evac, n0=0, kc_outer_first=False):
            j0 = 0
            if kc_outer_first:
                sls = [get_slab("fm"), get_slab("fm", hold=1)]
                fbanks = [mm.next() for _ in range(4)]
                for kc in range(KC):
                    rhs, ru = rhs_fn(kc)
                    last = kc == KC - 1
                    for n in range(4):
                        sl, slu = sls[n // 2]
                        c = n % 2
                        op(pe, lambda: nc.tensor.matmul(bank(fbanks[n]), lhsT=sl[:, kc * 256 + c * 128:kc * 256 + c * 128 + 128], rhs=rhs,
                                                        start=(kc == 0), stop=last),
                           reads=[slu, ru], writes=[PSu[fbanks[n]]], signal=last)
                for n in range(4):
                    evac(n0 + n, fbanks[n])
                    inject()
                j0 = 2
            for j in range(j0, nslabs):
                sl, slu = get_slab("fm")
                for c in range(2):
                    n = n0 + 2 * j + c
                    b = mm.next()
                    for kc in range(KC):
                        rhs, ru = rhs_fn(kc)
                        last = kc == KC - 1
                        op(pe, lambda: nc.tensor.matmul(bank(b), lhsT=sl[:, kc * 256 + c * 128:kc * 256 + c * 128 + 128], rhs=rhs,
                                                        start=(kc == 0), stop=last),
                           reads=[slu, ru], writes=[PSu[b]], signal=last)
                    evac(n, b)
                    inject()

        def tm_linear(n_nb, hb, evac):
            for nb in range(n_nb):
                banks = [mm.next() for _ in range(4)]
                for kh in range(2):
                    sl, slu = get_slab("tm")
                    for tc in range(4):
                        for k8 in range(8):
                            kc = kh * 8 + k8
                            last = kc == KC - 1
                            sig = last or (tc == 3 and k8 == 7)
                            op(pe, lambda: nc.tensor.matmul(bank(banks[tc]), lhsT=hcols(hb, kc)[:, tc * 128:(tc + 1) * 128],
                                                            rhs=sl[:, k8 * 512:(k8 + 1) * 512], start=(kc == 0), stop=last),
                               reads=[slu, Hu[hb][kc]], writes=[PSu[banks[tc]]], signal=sig)
                        inject()
                for tc in range(4):
                    evac(nb, tc, banks[tc])

        def h_rhs(hb):
            return lambda kc: (hcols(hb, kc), Hu[hb][kc])

        def resid_evac(n, b):
            op(dve, lambda: nc.vector.tensor_tensor(out=cols(Xb, n), in0=bank(b), in1=cols(Xb, n), op=ALU.add),
               reads=[PSu[b], Xu[n]], writes=[Xu[n]])

        def ffn(hb, next_gidx, next_to_x=False):
            acc = None
            for half in range(2):
                def ev1(n, b):
                    r = tmp.next()
                    op(act, lambda: nc.scalar.activation(out=cols(TMP, r), in_=bank(b), func=AF.Relu),
                       reads=[PSu[b]], writes=[TMPu[r]])
                    op(dve, lambda: nc.vector.tensor_tensor(out=cols(R1, n), in0=cols(TMP, r), in1=cols(TMP, r), op=ALU.mult),
                       reads=[TMPu[r]], writes=[R1u[n]])
                fm_linear(16, h_rhs(hb), ev1, kc_outer_first=(half == 0))
                for n in range(16):
                    sl, slu = get_slab("f2")
                    b = mm.next()
                    for kc in range(32):
                        op(pe, lambda: nc.tensor.matmul(bank(b), lhsT=sl[:, kc * 128:(kc + 1) * 128], rhs=cols(R1, kc),
                                                        start=(kc == 0), stop=(kc == 31)),
                           reads=[slu, R1u[kc]], writes=[PSu[b]], signal=(kc == 31))
                    if half == 1:
                        if acc is None:
                            acc = make_acc(next_gidx, next_to_x)
                        acc["evac"](n, b)
                    else:
                        resid_evac(n, b)
                    inject()
            return acc

        def dump_x(tt):
            ch = P.chan("dbg")
            dma(sp, ch, yT[:, tt], Xb[:], reads=Xu, writes=[])
            return ch

        out_ch = P.chan("out")
        chX = P.chan("X")
        chR = [P.chan(f"rope{i}") for i in range(4)]
        chQ, chK, chV = P.chan("Q"), P.chan("K"), P.chan("V")
        dbg_ch = []

        def load_x_A(tt):
            dma(sp, chX, Xb[:], xT[:, tt], writes=Xu)

        ntA = NT if dbg is None else 1
        hb_next = None
        for tt in range(ntA):
            if tt == 0:
                load_x_A(0)
                hb = rmsnorm(0)
            else:
                hb = hb_next

            def v_evac(nb, tc, b):
                v32 = R1[:, tc * 4096:(tc + 1) * 4096].bitcast(F32)
                op(act, lambda: nc.scalar.activation(out=v32[:, nb * 512:(nb + 1) * 512], in_=bank(b), func=AF.Gelu),
                   reads=[PSu[b]], writes=[R1u[tc * 8 + nb * 2], R1u[tc * 8 + nb * 2 + 1]])
            tm_linear(4, hb, v_evac)

            def ln_step(tc):
                def f():
                    v32 = R1[:, tc * 4096:(tc + 1) * 4096].bitcast(F32)
                    for nb in range(4):
                        op(dve, lambda: nc.vector.bn_stats(out=STATS[:, nb * 6:(nb + 1) * 6], in_=v32[:, nb * 512:(nb + 1) * 512]),
                           reads=[R1u[tc * 8 + nb * 2], R1u[tc * 8 + nb * 2 + 1]], writes=[u_stats])
                    op(dve, lambda: nc.vector.bn_aggr(out=MV[:], in_=STATS[:]), reads=[u_stats], writes=[u_mv])
                    op(act, lambda: nc.scalar.activation(out=SM[:, 0:1], in_=MV[:, 1:2], func=AF.Ln, bias=EPSC[:, 0:1], scale=1.0),
                       reads=[u_mv, u_negh], writes=[u_sm])
                    op(act, lambda: nc.scalar.activation(out=SM[:, 1:2], in_=SM[:, 0:1], func=AF.Exp, scale=-0.5),
                       reads=[u_sm], writes=[u_sm])
                    op(dve, lambda: nc.vector.tensor_scalar(out=cols(P2, 16 + 4 * tc, 4), in0=v32, scalar1=MV[:, 0:1], scalar2=SM[:, 1:2],
                                                            op0=ALU.subtract, op1=ALU.mult),
                       reads=R1u[tc * 8:tc * 8 + 8] + [u_mv, u_sm], writes=P2u[16 + 4 * tc:16 + 4 * tc + 4])
                return f
            for tc in range(4):
                bg.append(ln_step(tc))
                bg.append(lambda: None)
            fm_linear(8, h_rhs(hb),
                      lambda n, b: op(act, lambda: nc.scalar.activation(out=cols(P2, n), in_=bank(b), func=AF.Gelu),
                                      reads=[PSu[b]], writes=[P2u[n]]))
            flush_bg()
            sl01 = [get_slab("fm"), get_slab("fm", hold=1)]
            wbanks = [mm.next() for _ in range(4)]
            sbanks = {}

            def spatial(gq):
                for tc in range(4):
                    vn = cols(P2, 16 + 4 * tc, 4)
                    b = aux.next()
                    sbanks[(gq, tc)] = b
                    for j in range(4):
                        g = gq * 4 + j
                        op(pe, lambda: nc.tensor.matmul(bank(b, j * 128, 128), lhsT=vn[:, g * 128:(g + 1) * 128],
                                                        rhs=WST[:, g * 128:(g + 1) * 128], start=True, stop=True),
                           reads=[P2u[16 + 4 * tc + gq], u_wst], writes=[PSu[b]], signal=(j == 3))

            def gating(gq):
                for tc in range(4):
                    b = sbanks[(gq, tc)]
                    t = tmp.next()
                    for j in range(4):
                        g = gq * 4 + j
                        op(dve, lambda: nc.vector.scalar_tensor_tensor(
                            out=TMP[:, t * T + j * 128:t * T + (j + 1) * 128], in0=bank(b, j * 128, 128), scalar=LNCOL[:, g:g + 1],
                            in1=BIAS2[:, g * 128:(g + 1) * 128], op0=ALU.mult, op1=ALU.add),
                           reads=[PSu[b], u_lncol, u_bias2], writes=[TMPu[t]])
                    uview = cols(P2, gq * 4, 4).rearrange("p (g t) -> p g t", t=T)[:, :, tc * 128:(tc + 1) * 128]
                    tview = cols(TMP, t).rearrange("p (g t) -> p g t", t=128)
                    op(dve, lambda: nc.vector.tensor_tensor(out=uview, in0=tview, in1=uview, op=ALU.mult),
                       reads=[TMPu[t]] + P2u[gq * 4:gq * 4 + 4], writes=P2u[gq * 4:gq * 4 + 4])

            def wout_stage(gq):
                for n in range(4):
                    sl, slu = sl01[n // 2]
                    for kc in range(4 * gq, 4 * gq + 4):
                        last = kc == KC - 1
                        c = n % 2
                        op(pe, lambda: nc.tensor.matmul(bank(wbanks[n]), lhsT=sl[:, kc * 256 + c * 128:kc * 256 + c * 128 + 128],
                                                        rhs=cols(P2, kc), start=(kc == 0), stop=last),
                           reads=[slu, P2u[kc]], writes=[PSu[wbanks[n]]], signal=last)

            spatial(0)
            for gq in range(4):
                gating(gq)
                if gq + 1 < 4:
                    spatial(gq + 1)
                wout_stage(gq)
            acc1 = make_acc(1)
            for n in range(4):
                acc1["evac"](n, wbanks[n])
            fm_linear(6, lambda kc: (cols(P2, kc), P2u[kc]), acc1["evac"], n0=4)
            if dbg == "mix0":
                dbg_ch.append(dump_x(tt))
                break
            hb1 = acc1["finish"]()

            def rope_load(tt=tt):
                for i in range(4):
                    dma(sp, chR[i], cols(ROPE, i), rope_d[:, i % 2, tt * T:(tt + 1) * T], writes=[ROPEu[i]])

            def rope_scale():
                for i in range(4):
                    op(dve, lambda: nc.vector.tensor_scalar(out=cols(ROPE, i), in0=cols(ROPE, i), scalar1=QKG[:, i:i + 1], scalar2=None,
                                                            op0=ALU.mult),
                       reads=[ROPEu[i], u_qkg], writes=[ROPEu[i]])
            bg.append(rope_load)
            bg.extend([lambda: None] * 8)
            bg.append(rope_scale)
            acc2 = ffn(hb1, 2)
            flush_bg()
            dma(sp, chX, Xs[tt], Xb[:], reads=Xu, writes=[Xs_u[tt]])
            if dbg == "ffn0":
                dbg_ch.append(dump_x(tt))
                break

            hb2 = acc2["finish"]()
            if tt + 1 < ntA:
                hb_next, nsteps = rmsnorm_steps(0)
                bg.append(lambda: load_x_A(tt + 1))
                bg.extend([lambda: None] * 5)
                bg.extend(nsteps)
            pend = []

            def qk_post(kind):
                ci = 0 if kind == "q" else 2

                def ev(n, b):
                    outu = n if kind == "q" else 16 + n
                    q32 = q32r.next()
                    op(act, lambda: nc.scalar.activation(out=cols(TMP, q32), in_=bank(b), func=AF.Copy),
                       reads=[PSu[b]], writes=[TMPu[q32]])
                    sq = sqr.next()
                    op(act, lambda: nc.scalar.activation(out=cols(TB, sq), in_=bank(b), func=AF.Square),
                       reads=[PSu[b]], writes=[TBu[sq]])
                    hi, lo = hlr.next()
                    op(act, lambda: nc.scalar.activation(out=cols(TB, hi), in_=bank(b), func=AF.Copy),
                       reads=[PSu[b]], writes=[TBu[hi]])
                    op(pool, lambda: nc.gpsimd.tensor_tensor(out=cols(TB, lo), in0=cols(TMP, q32), in1=cols(TB, hi), op=ALU.subtract),
                       reads=[TMPu[q32], TBu[hi]], writes=[TBu[lo]])

                    def stage2():
                        ss = aux.next()
                        op(pe, lambda: nc.tensor.matmul(bank(ss), lhsT=ONES[:], rhs=cols(TB, sq), start=True, stop=True),
                           reads=[TBu[sq], u_ones], writes=[PSu[ss]])
                        pp = aux.next()
                        op(pe, lambda: nc.tensor.matmul(bank(pp), lhsT=PM[:], rhs=cols(TB, hi), start=True, stop=False),
                           reads=[TBu[hi], u_pm], writes=[PSu[pp]], signal=False)
                        op(pe, lambda: nc.tensor.matmul(bank(pp), lhsT=PM[:], rhs=cols(TB, lo), start=False, stop=True),
                           reads=[TBu[lo], u_pm], writes=[PSu[pp]])
                        r = rstd_from(ss, 1.0 / 128)
                        t = tmp.next()
                        a = tmp.next()
                        op(dve, lambda: nc.vector.tensor_tensor(out=cols(TMP, a), in0=cols(TMP, q32), in1=cols(ROPE, ci), op=ALU.mult),
                           reads=[TMPu[q32], ROPEu[ci]], writes=[TMPu[a]])
                        op(dve, lambda: nc.vector.tensor_tensor(out=cols(TMP, t), in0=bank(pp), in1=cols(ROPE, ci + 1), op=ALU.mult),
                           reads=[PSu[pp], ROPEu[ci + 1], TMPu[t]], writes=[TMPu[t]])
                        op(dve, lambda: nc.vector.tensor_tensor(out=cols(TMP, a), in0=cols(TMP, a), in1=cols(TMP, t), op=ALU.add),
                           reads=[TMPu[a], TMPu[t]], writes=[TMPu[a]])
                        op(dve, lambda: nc.vector.tensor_tensor(out=cols(R1, outu), in0=cols(TMP, a), in1=cols(TMP, r), op=ALU.mult),
                           reads=[TMPu[a], TMPu[r]], writes=[R1u[outu]])
                    if pend:
                        pend.pop()()
                    pend.append(stage2)
                return ev

            fm_linear(8, h_rhs(hb2), qk_post("q"), kc_outer_first=True)
            fm_linear(2, h_rhs(hb2), qk_post("k"))
            if pend:
                pend.pop()()
            tm_linear(1, hb2, lambda nb, tc, b: op(act, lambda: nc.scalar.activation(out=cols(R1, 20 + tc), in_=bank(b), func=AF.Copy),
                                                   reads=[PSu[b]], writes=[R1u[20 + tc]]))
            flush_bg()
            dma(sp, chQ, Qs[tt], cols(R1, 0, 16), reads=R1u[0:16], writes=[Qs_u[tt]])
            dma(sp, chK, Ks[:, :, tt * T:(tt + 1) * T], cols(R1, 16, 4).rearrange("p (h t) -> p h t", t=T),
                reads=R1u[16:20], writes=[Ks_u[tt]])
            dma(sp, chV, Vs[:, tt * 4:(tt + 1) * 4, :], cols(R1, 20, 4).rearrange("p (c f) -> p c f", f=512),
                reads=R1u[20:24], writes=[Vs_u[tt]])

        ntB = NT if dbg is None else 0
        if ntB:
            chKV = [P.chan("KVa"), P.chan("KVb")]
            dma(sp, chKV[0], cols(P2, 0, 16), Ks.rearrange("p h t -> p (h t)"), reads=Ks_u, writes=P2u[0:16])
            dma(sp, chKV[1], cols(P2, 16, 16), Vs.rearrange("p c f -> p (c f)"), reads=Vs_u, writes=P2u[16:32])
            dma(sp, chQ, cols(R1, 0, 16), Qs[0], reads=[Qs_u[0]], writes=R1u[0:16])
            dma(sp, chX, Xb[:], Xs[0], reads=[Xs_u[0]], writes=Xu)
        for tt in range(ntB):
            NG = 16 * 8

            def s_pair(g):
                h, j = divmod(g, 8)
                kvh = h // 4
                sb0 = 2 * (g % 2)
                for i in range(2):
                    kc = 2 * j + i
                    op(pe, lambda: nc.tensor.matmul(bank(sb0 + i), lhsT=P2[:, kvh * S + kc * 128:kvh * S + (kc + 1) * 128], rhs=cols(R1, h),
                                                    start=True, stop=True),
                       reads=[P2u[kvh * 4 + kc // 4], R1u[h]], writes=[PSu[sb0 + i]], signal=(i == 1))

            def exp_pair(g):
                sb0 = 2 * (g % 2)
                p0 = 2 * (g % 3)
                op(act, lambda: nc.scalar.activation(out=cols(TB, p0, 2), in_=PS[:, sb0 * 512:(sb0 + 2) * 512], func=AF.Exp, scale=SCALE),
                   reads=[PSu[sb0], PSu[sb0 + 1]], writes=[TBu[p0], TBu[p0 + 1]])

            def od_pair(g):
                h, j = divmod(g, 8)
                kvh = h // 4
                p0 = 2 * (g % 3)
                ob, db = 4 + h % 2, 6 + h % 2
                for i in range(2):
                    kc = 2 * j + i
                    op(pe, lambda: nc.tensor.matmul(bank(ob), lhsT=P2[:, (16 + kc) * T + kvh * 128:(16 + kc) * T + (kvh + 1) * 128],
                                                    rhs=cols(TB, p0 + i), start=(kc == 0), stop=(kc == 15)),
                       reads=[P2u[16 + kc], TBu[p0 + i]], writes=[PSu[ob]], signal=(kc == 15))
                    op(pe, lambda: nc.tensor.matmul(bank(db), lhsT=ONES[:], rhs=cols(TB, p0 + i), start=(kc == 0), stop=(kc == 15)),
                       reads=[TBu[p0 + i], u_ones], writes=[PSu[db]], signal=(i == 1))
                if j == 7:
                    rd = tmp.next()
                    op(dve, lambda: nc.vector.reciprocal(out=cols(TMP, rd), in_=bank(db)), reads=[PSu[db]], writes=[TMPu[rd]])
                    op(dve, lambda: nc.vector.tensor_tensor(out=cols(R1, 16 + h), in0=bank(ob), in1=cols(TMP, rd), op=ALU.mult),
                       reads=[PSu[ob], TMPu[rd]], writes=[R1u[16 + h]])
                    inject()

            s_pair(0)
            for g in range(NG + 1):
                if g < NG:
                    exp_pair(g)
                if g + 1 < NG:
                    s_pair(g + 1)
                if g >= 1:
                    od_pair(g - 1)
            flush_bg()
            acc3 = make_acc(3)
            fm_linear(8, lambda kc: (cols(R1, 16 + kc), R1u[16 + kc]), acc3["evac"])
            hb3 = acc3["finish"]()
            acc4 = ffn(hb3, 4, True)
            last = tt + 1 == ntB
            if not last:
                dma(sp, chQ, cols(R1, 0, 16), Qs[tt + 1], reads=[Qs_u[tt + 1]], writes=R1u[0:16])
            fsteps = acc4["tail_steps"]()
            fsteps.pop(0)()

            def store_step(tt=tt):
                dma(sp, out_ch, yT[:, tt], Xb[:], reads=Xu, writes=[])

            def xload_step(tt=tt):
                dma(sp, chX, Xb[:], Xs[tt + 1], reads=[Xs_u[tt + 1]], writes=Xu)
            fsteps.append(store_step)
            if not last:
                fsteps.append(xload_step)
                bg.extend(fsteps)
            else:
                for f in fsteps:
                    f()

        for ch in [out_ch] + dbg_ch:
            if ch.count:
                sp.e.wait_ge(ch.sem, ch.count)
        for e in (pe, act, dve, pool):
            if e.count:
                sp.e.wait_ge(e.sem, e.count)
    return nc


def rope_tables():
    half = 32
    inv_freq = (10000.0 ** (-np.arange(0, 64, 2, dtype=np.float32) / np.float32(64))).astype(np.float32)
    t = np.arange(S)
    row = (t // 64).astype(np.float32)
    col = (t % 64).astype(np.float32)
    cos = np.zeros((128, S), np.float32)
    sin = np.zeros((128, S), np.float32)
    for d in range(128):
        pos = row if d < 64 else col
        j = d % 64
        ang = (pos * inv_freq[j % half]).astype(np.float32)
        cos[d] = np.cos(ang)
        sgn = -1.0 if j < half else 1.0
        sin[d] = sgn * np.sin(ang)
    return np.stack([cos, sin], axis=1).astype(np.float32)


def partner_perm():
    p = np.arange(128)
    j = p % 64
    return np.where(j < 32, p + 32, p - 32)


def prep_shared(inputs):
    W = {k: np.asarray(v, dtype=np.float32) for k, v in inputs.items() if k != "x"}
    specA, specB = slab_specs()
    wsl = np.empty((len(specA) + len(specB), 128, SLAB), np.float32)
    for i, sp_ in enumerate(specA + specB):
        wsl[i] = pack_slab(sp_, W)

    def colmajor(v):
        return v.reshape(KC, 128).T

    gains = np.concatenate([colmajor(W["norm_mix"][0]), colmajor(W["norm_ffn"][0]), colmajor(W["norm_mix"][1]),
                            colmajor(W["norm_ffn"][1]), colmajor(W["norm_final"])], axis=1)
    lncol = np.concatenate([colmajor(W["gm_ln_g"][0]), colmajor(W["gm_ln_b"][0])], axis=1)
    wsT = W["gm_w_s"][0].transpose(2, 0, 1).reshape(128, 16 * 128)
    bsb = np.broadcast_to(W["gm_b_s"][0].reshape(1, 16 * 128), (128, 16 * 128))
    perm = partner_perm()
    gq, gk = W["attn_q_norm"][0], W["attn_k_norm"][0]
    qkg = np.stack([gq, gq[perm], gk, gk[perm]], axis=1)
    pm = np.zeros((128, 128), np.float32)
    pm[perm, np.arange(128)] = 1.0
    c = np.ascontiguousarray
    return {"wsl": wsl, "gains": c(gains, np.float32), "lncol": c(lncol, np.float32), "wsT": c(wsT, np.float32),
            "bsb": c(bsb, np.float32), "qkg": c(qkg, np.float32), "pm": pm, "rope": rope_tables()}


def to_fm(xb):
    return np.ascontiguousarray(xb.reshape(NT, T, KC, 128).transpose(3, 0, 2, 1).reshape(128, NT, KC * T))


def from_fm(y):
    return np.ascontiguousarray(y.reshape(128, NT, KC, T).transpose(1, 3, 2, 0).reshape(S, D))


def kernel(**inputs):
    x = np.asarray(inputs["x"], dtype=np.float32)
    shared = prep_shared(inputs)
    nc = build_program(DBG)
    in_maps = []
    for b in range(8):
        m = dict(shared)
        m["xT"] = to_fm(x[b])
        in_maps.append(m)
    res = run_bass_kernel_spmd(nc, in_maps, core_ids=list(range(8)))
    out = np.stack([from_fm(np.asarray(res.results[b]["yT"], dtype=np.float32)) for b in range(8)], axis=0)
    return out.astype(np.float32)
```
